# Optimizing a Trainium2 kernel written in Bass

```python
import jax, jax.numpy as jnp
from jax import lax
import numpy as np

D_MODEL = 1024
BATCH = 16
SEQ = 256
DEPTH = 2
DEC_BATCH = 2
DEC_SEQ = 2048
PAST_LEN = 256

GRID_W = 64
EPS = 1e-6
N_BRANCH = 3
BRANCH_DIM = 512
CONV_DIM = BRANCH_DIM
CONV_K = 3
N_HEADS = 8
QK_NOPE = 64
QK_ROPE = 32
V_HEAD = 64
Q_LORA = 384
KV_LORA = 256
MLA_LATENT = KV_LORA + QK_ROPE
MLA_V_WIDTH = N_HEADS * V_HEAD
ROPE_THETA = 10000.0
Q_BLOCK = 128
POOL_WINDOWS = (2, 4, 8, 16)
N_POOL = 4
POOL_GROUP = 128
POOL_DIM = N_POOL * POOL_GROUP
SPLIT_SIZES = (CONV_DIM, CONV_DIM, CONV_DIM, CONV_DIM, Q_LORA, MLA_LATENT, MLA_V_WIDTH, POOL_DIM, POOL_DIM, N_BRANCH * D_MODEL)
W_IN_COLS = sum(SPLIT_SIZES)

kernel_name = 'hybrid_diffusion_conv_mla_pool_step'


def rms_norm(x, g):
    xf = x.astype(jnp.float32)
    y = xf * lax.rsqrt(jnp.mean(xf * xf, axis=-1, keepdims=True) + EPS)
    return (y * g.astype(jnp.float32)).astype(x.dtype)


def axial_rope_tables(n_tokens):
    t = jnp.arange(n_tokens)
    row = (t // GRID_W).astype(jnp.float32)
    col = (t % GRID_W).astype(jnp.float32)
    axis_dim = QK_ROPE // 2
    inv = 1.0 / (ROPE_THETA ** (jnp.arange(0, axis_dim, 2, dtype=jnp.float32) / axis_dim))
    ang = jnp.stack([row[:, None] * inv, col[:, None] * inv], axis=1)
    return jnp.cos(ang), jnp.sin(ang)


def apply_axial_rope(x, cos, sin):
    xs = x.reshape(x.shape[:-1] + (2, 2, QK_ROPE // 4))
    x1, x2 = xs[..., 0, :], xs[..., 1, :]
    c = cos[:, None].astype(x.dtype)
    s = sin[:, None].astype(x.dtype)
    out = jnp.stack([x1 * c - x2 * s, x2 * c + x1 * s], axis=-2)
    return out.reshape(x.shape)


def blocked_attention(q, k, v):
    b, sq, h, dk = q.shape
    nb = sq // Q_BLOCK
    qb = q.reshape(b, nb, Q_BLOCK, h, dk).transpose(1, 0, 2, 3, 4)
    scale = dk ** -0.5

    def one_block(qblk):
        s = jnp.einsum('bqhd,bkhd->bhqk', qblk, k).astype(jnp.float32) * scale
        p = jax.nn.softmax(s, axis=-1).astype(v.dtype)
        return jnp.einsum('bhqk,bkhd->bqhd', p, v)

    o = lax.map(one_block, qb)
    return o.transpose(1, 0, 2, 3, 4).reshape(b, sq, h, v.shape[-1])


def centred_conv3(u, w, bias):
    up = jnp.pad(u, ((0, 0), (1, 1), (0, 0)))
    return up[:, :-2] * w[0] + up[:, 1:-1] * w[1] + up[:, 2:] * w[2] + bias


def multiscale_pool(u, pool_w, pool_scale):
    b, s, _ = u.shape
    ug = u.reshape(b, s, N_POOL, POOL_GROUP)
    cs = jnp.pad(jnp.cumsum(ug.astype(jnp.float32), axis=1), ((0, 0), (1, 0), (0, 0), (0, 0)))
    t = np.arange(s)
    means = []
    for gi, win in enumerate(POOL_WINDOWS):
        lo = np.clip(t - win // 2, 0, s)
        hi = np.clip(t + win - win // 2, 0, s)
        total = cs[:, hi, gi, :] - cs[:, lo, gi, :]
        means.append(total / (hi - lo).astype(np.float32)[:, None])
    pooled = jnp.stack(means, axis=2).astype(u.dtype) - ug
    mixed = jnp.einsum('bsgc,gcd->bsgd', pooled, pool_w).reshape(b, s, POOL_DIM)
    return mixed * pool_scale


def trunk_layer(x, cond, w_mod, b_mod, g_pre, g_post, w_in, conv_w, conv_b, g_q, w_uq, g_kv, w_ukv,
                pool_w, pool_scale, w_branch, w_o, ctx_latent, cos, sin):
    b, s, _ = x.shape
    mod = jax.nn.silu(cond) @ w_mod + b_mod
    shift, scale, gate = jnp.split(mod[:, None, :], 3, axis=-1)
    hn = rms_norm(x, g_pre) * (1 + scale) + shift
    proj = hn @ w_in
    split_points = np.cumsum(SPLIT_SIZES)[:-1].tolist()
    (a_b, a_c, a_x, a_z, q_down, kv_down, b_z, c_u, c_z, merge_logits) = jnp.split(proj, split_points, axis=-1)

    y_a = jax.nn.silu(a_z) * (a_b * centred_conv3(a_c * a_x, conv_w, conv_b))

    q = (rms_norm(q_down, g_q) @ w_uq).reshape(b, s, N_HEADS, QK_NOPE + QK_ROPE)
    q_nope, q_rope = q[..., :QK_NOPE], q[..., QK_NOPE:]
    own_latent = jnp.concatenate([rms_norm(kv_down[..., :KV_LORA], g_kv), kv_down[..., KV_LORA:]], axis=-1)
    ckv = own_latent[..., :KV_LORA]
    krope = own_latent[..., None, KV_LORA:]
    if cos is not None:
        q_rope = apply_axial_rope(q_rope, cos, sin)
        krope = apply_axial_rope(krope, cos, sin)
    if ctx_latent is not None:
        ckv = jnp.concatenate([ctx_latent[..., :KV_LORA], ckv], axis=1)
        krope = jnp.concatenate([ctx_latent[..., None, KV_LORA:], krope], axis=1)
    sk = ckv.shape[1]
    kv = (ckv @ w_ukv).reshape(b, sk, N_HEADS, QK_NOPE + V_HEAD)
    k = jnp.concatenate([kv[..., :QK_NOPE], jnp.broadcast_to(krope, (b, sk, N_HEADS, QK_ROPE))], axis=-1)
    qf = jnp.concatenate([q_nope, q_rope], axis=-1)
    attn = blocked_attention(qf, k, kv[..., QK_NOPE:]).reshape(b, s, MLA_V_WIDTH)
    y_b = jax.nn.silu(b_z) * attn

    y_c = jax.nn.silu(c_z) * multiscale_pool(c_u, pool_w, pool_scale)

    branches = jnp.einsum('bsnc,ncd->bsnd', jnp.stack([y_a, y_b, y_c], axis=2), w_branch)
    gates = jax.nn.sigmoid(merge_logits.astype(jnp.float32)).astype(x.dtype).reshape(b, s, N_BRANCH, D_MODEL)
    merged = jnp.sum(gates * branches, axis=2)
    out = rms_norm(merged @ w_o, g_post)
    return x + gate * out, own_latent


def setup_inputs(seed: int = 0) -> dict:
    key = jax.random.key(seed)
    ks = jax.random.split(key, 24)

    def nrm(k, shape, s):
        return jax.random.normal(k, shape, jnp.float32) * s

    def gain(k, shape):
        return 1.0 + 0.1 * jax.random.normal(k, shape, jnp.float32)

    return {
        'x_prompt': nrm(ks[0], (BATCH, SEQ, D_MODEL), 1.0),
        'x_sample': nrm(ks[1], (DEC_BATCH, DEC_SEQ, D_MODEL), 1.0),
        'cache_mla_latent': nrm(ks[2], (DEC_BATCH, DEPTH, PAST_LEN, MLA_LATENT), 1.0),
        'c': nrm(ks[3], (DEC_BATCH, D_MODEL), 1.0),
        'c_ctx': nrm(ks[4], (D_MODEL,), 1.0),
        'w_mod': nrm(ks[5], (DEPTH, D_MODEL, 3 * D_MODEL), 0.5 * D_MODEL ** -0.5),
        'b_mod': nrm(ks[6], (DEPTH, 3 * D_MODEL), 0.02),
        'g_pre': gain(ks[7], (DEPTH, D_MODEL)),
        'g_post': gain(ks[8], (DEPTH, D_MODEL)),
        'w_in': nrm(ks[9], (DEPTH, D_MODEL, W_IN_COLS), D_MODEL ** -0.5),
        'conv_w': nrm(ks[10], (DEPTH, CONV_K, CONV_DIM), CONV_K ** -0.5),
        'conv_b': nrm(ks[11], (DEPTH, CONV_DIM), 0.01),
        'g_q': gain(ks[12], (DEPTH, Q_LORA)),
        'w_uq': nrm(ks[13], (DEPTH, Q_LORA, N_HEADS * (QK_NOPE + QK_ROPE)), Q_LORA ** -0.5),
        'g_kv': gain(ks[14], (DEPTH, KV_LORA)),
        'w_ukv': nrm(ks[15], (DEPTH, KV_LORA, N_HEADS * (QK_NOPE + V_HEAD)), KV_LORA ** -0.5),
        'pool_w': nrm(ks[16], (DEPTH, N_POOL, POOL_GROUP, POOL_GROUP), POOL_GROUP ** -0.5),
        'pool_scale': gain(ks[17], (DEPTH, POOL_DIM)),
        'w_branch': nrm(ks[18], (DEPTH, N_BRANCH, BRANCH_DIM, D_MODEL), BRANCH_DIM ** -0.5),
        'w_o': nrm(ks[19], (DEPTH, D_MODEL, D_MODEL), D_MODEL ** -0.5),
    }


def reference(x_prompt, x_sample, cache_mla_latent, c, c_ctx, w_mod, b_mod, g_pre, g_post, w_in,
              conv_w, conv_b, g_q, w_uq, g_kv, w_ukv, pool_w, pool_scale, w_branch, w_o):
    h = x_prompt
    cond_ctx = jnp.broadcast_to(c_ctx, (x_prompt.shape[0], D_MODEL))
    ctx_states = []
    for l in range(DEPTH):
        h, lat = trunk_layer(h, cond_ctx, w_mod[l], b_mod[l], g_pre[l], g_post[l], w_in[l], conv_w[l], conv_b[l],
                             g_q[l], w_uq[l], g_kv[l], w_ukv[l], pool_w[l], pool_scale[l], w_branch[l], w_o[l],
                             None, None, None)
        ctx_states.append(lat)
    state_mla_latent = jnp.stack(ctx_states, axis=1)

    cos, sin = axial_rope_tables(x_sample.shape[1])
    hs = x_sample
    for l in range(DEPTH):
        hs, _ = trunk_layer(hs, c, w_mod[l], b_mod[l], g_pre[l], g_post[l], w_in[l], conv_w[l], conv_b[l],
                            g_q[l], w_uq[l], g_kv[l], w_ukv[l], pool_w[l], pool_scale[l], w_branch[l], w_o[l],
                            cache_mla_latent[:, l], cos, sin)
    return (h, hs, state_mla_latent)
```

```python
import numpy as np
import ml_dtypes
import concourse.bass as bass
import concourse.mybir as mybir
from concourse.bass_utils import run_bass_kernel_spmd

F32 = mybir.dt.float32
BF16 = mybir.dt.bfloat16
I32 = mybir.dt.int32
AF = mybir.ActivationFunctionType
ALU = mybir.AluOpType

D_MODEL = 1024
DEPTH = 2
EPS = 1e-6
W_IN_COLS = 7328
OFF_AB, OFF_AC, OFF_AX, OFF_AZ, OFF_Q, OFF_KV, OFF_BZ, OFF_CU, OFF_CZ, OFF_MG = (
    0, 512, 1024, 1536, 2048, 2432, 2720, 3232, 3744, 4256)
NK_S = 2304
SM_SCALE = 96 ** -0.5
POOL_WINDOWS = (2, 4, 8, 16)

ENGS = ("pe", "act", "dve", "pool", "sp")
DMA_RING = 8

W_BLOCKS = [("kv", OFF_KV, 288), ("cu", OFF_CU, 512), ("ac", OFF_AC, 512), ("ax", OFF_AX, 512), ("q", OFF_Q, 384),
            ("az", OFF_AZ, 512), ("ab", OFF_AB, 512), ("bz", OFF_BZ, 512), ("cz", OFF_CZ, 512)]


def _pack_layout():
    off = {}
    cur = 0

    def add(name, n):
        nonlocal cur
        off[name] = (cur, n)
        cur += n
    for nm, c0, W in W_BLOCKS:
        add("win_" + nm, 8 * W)
    for dc in range(8):
        add("gate%d" % dc, 8 * 3 * 128)
        add("br%d" % dc, 3 * 4 * 128)
    for half in range(2):
        add("wo%d" % half, 8 * 512)
    add("modA", 8 * 512)
    add("modB", 8 * 256)
    add("wuq", 3 * 768)
    add("wukv", 2 * 1024)
    add("poolw", 4 * 128)
    return off, cur


PK_OFF, PK_COLS = _pack_layout()


def _pack_mod(out, w_mod, r):
    for l in range(2):
        wm = w_mod[l].reshape(8, 128, 3, 1024)[:, :, :, r * 256:(r + 1) * 256]
        o, n = PK_OFF["modA"]
        out[l, :, o:o + n] = wm[:, :, 0:2, :].transpose(1, 0, 2, 3).reshape(128, n)
        o, n = PK_OFF["modB"]
        out[l, :, o:o + n] = wm[:, :, 2, :].transpose(1, 0, 2).reshape(128, n)


def _pack_weights(w_mod, w_in, w_uq, w_ukv, pool_w, w_branch, w_o):
    out = np.empty((2, 128, PK_COLS), np.float32)
    for l in range(2):
        def put(name, arr):
            o, n = PK_OFF[name]
            out[l, :, o:o + n] = arr.reshape(128, n)
        win = w_in[l].reshape(8, 128, W_IN_COLS)
        for nm, c0, W in W_BLOCKS:
            put("win_" + nm, win[:, :, c0:c0 + W].transpose(1, 0, 2))
        mg = win[:, :, OFF_MG:OFF_MG + 3072].reshape(8, 128, 3, 8, 128)
        wb = w_branch[l].reshape(3, 4, 128, 8, 128)
        for dc in range(8):
            put("gate%d" % dc, mg[:, :, :, dc, :].transpose(1, 0, 2, 3))
            put("br%d" % dc, wb[:, :, :, dc, :].transpose(2, 0, 1, 3))
        wo = w_o[l].reshape(8, 128, 1024)
        for half in range(2):
            put("wo%d" % half, wo[:, :, half * 512:(half + 1) * 512].transpose(1, 0, 2))
        put("wuq", w_uq[l].reshape(3, 128, 768).transpose(1, 0, 2))
        put("wukv", w_ukv[l].reshape(2, 128, 1024).transpose(1, 0, 2))
        put("poolw", pool_w[l].transpose(1, 0, 2))
    return out


class _Rec:
    def __init__(self):
        self.call = None

    def __getattr__(self, name):
        def f(*a, **kw):
            assert self.call is None
            self.call = (name, a, kw)
            return self
        return f


class _Op:
    __slots__ = ("eng", "fn", "reads", "writes", "kind", "deps", "idx", "sig", "sem", "val", "clock")

    def __init__(self, eng, fn, reads, writes, kind):
        rec = _Rec()
        fn(rec)
        name, a, kw = rec.call
        self.eng, self.reads, self.writes, self.kind = eng, reads, writes, kind
        self.fn = lambda e: getattr(e, name)(*a, **kw)
        self.deps, self.sig, self.sem, self.val, self.clock = [], False, None, 0, None


class Prog:
    def __init__(self, nc):
        self.nc = nc
        self.ops = []
        self.last_w = {}
        self.readers = {}
        self.engobj = {"pe": nc.tensor, "act": nc.scalar, "dve": nc.vector, "pool": nc.gpsimd, "sp": nc.sync}

    def op(self, eng, fn, reads=(), writes=(), kind="c"):
        o = _Op(eng, fn, tuple(reads), tuple(writes), kind)
        o.idx = len(self.ops)
        deps = set()
        for r in o.reads:
            w = self.last_w.get(r)
            if w is not None:
                deps.add(w)
            if r.startswith("ps"):
                for rd in self.readers.get(r, ()):
                    if self.ops[rd].eng != eng:
                        deps.add(rd)
        for w_ in o.writes:
            w = self.last_w.get(w_)
            if w is not None:
                deps.add(w)
            deps.update(self.readers.get(w_, ()))
        deps.discard(o.idx)
        latest = {}
        keep = []
        for d in deps:
            p = self.ops[d]
            if p.kind == "c":
                if p.eng not in latest or latest[p.eng] < d:
                    latest[p.eng] = d
            else:
                keep.append(d)
        o.deps = sorted(keep + list(latest.values()))
        for r in o.reads:
            self.readers.setdefault(r, []).append(o.idx)
        for w_ in o.writes:
            self.last_w[w_] = o.idx
            self.readers[w_] = []
        self.ops.append(o)
        return o

    def pe(self, fn, reads=(), writes=()):
        return self.op("pe", fn, reads, writes)

    def act(self, fn, reads=(), writes=()):
        return self.op("act", fn, reads, writes)

    def dve(self, fn, reads=(), writes=()):
        return self.op("dve", fn, reads, writes)

    def dma(self, eng, fn, reads=(), writes=()):
        return self.op(eng, fn, reads, writes, "d")

    def cc(self, fn, reads=(), writes=()):
        return self.op("pool", fn, reads, writes, "cc")

    @staticmethod
    def _skip(p, o):
        return p.eng == "pe" and o.eng == "pe" and p.kind == "c" and o.kind == "c"

    def emit(self, final_wait_ops=()):
        nc, ops = self.nc, self.ops
        for o in ops:
            for d in o.deps:
                if not self._skip(ops[d], o):
                    ops[d].sig = True
        for i in final_wait_ops:
            ops[i].sig = True
        esem = {e: nc.alloc_semaphore("s_" + e) for e in ENGS}
        ccsem = nc.alloc_semaphore("s_cc")
        rings = {e: [nc.alloc_semaphore("r_%s%d" % (e, i)) for i in range(DMA_RING)] for e in ("sp", "pool", "act")}
        ecount = {e: 0 for e in ENGS}
        cccount = 0
        dcount = {e: 0 for e in rings}
        known = {e: {} for e in ENGS}
        nwaits = 0

        def wait(eng, sem, val, clock):
            nonlocal nwaits
            k = known[eng]
            if k.get(sem.name, 0) >= val:
                return
            self.engobj[eng].wait_ge(sem, val)
            nwaits += 1
            if clock is not None:
                for s, v in clock.items():
                    if k.get(s, 0) < v:
                        k[s] = v
            if k.get(sem.name, 0) < val:
                k[sem.name] = val

        for o in ops:
            e = o.eng
            if o.kind == "d":
                i = dcount[e]
                s = rings[e][i % DMA_RING]
                prev = (i // DMA_RING) * 16
                if prev > 0:
                    wait(e, s, prev, None)
            for d in o.deps:
                p = ops[d]
                if self._skip(p, o):
                    continue
                wait(e, p.sem, p.val, p.clock)
            ins = o.fn(self.engobj[e])
            if o.kind == "d":
                i = dcount[e]
                s = rings[e][i % DMA_RING]
                v = (i // DMA_RING + 1) * 16
                ins.then_inc(s, 16)
                dcount[e] += 1
                o.sem, o.val = s, v
                ck = dict(known[e])
                ck[s.name] = v
                o.clock = ck
            elif o.kind == "cc":
                cccount += 1
                ins.then_inc(ccsem, 1)
                o.sem, o.val = ccsem, cccount
                ck = dict(known[e])
                ck[ccsem.name] = cccount
                o.clock = ck
            elif o.sig:
                ecount[e] += 1
                ins.then_inc(esem[e], 1)
                o.sem, o.val = esem[e], ecount[e]
                ck = dict(known[e])
                ck[esem[e].name] = ecount[e]
                o.clock = ck
        for i in final_wait_ops:
            p = ops[i]
            wait("sp", p.sem, p.val, p.clock)
        for e in rings:
            for k, s in enumerate(rings[e]):
                n = (dcount[e] - k + DMA_RING - 1) // DMA_RING if dcount[e] > k else 0
                if n > 0:
                    wait("sp", s, 16 * n, None)
        if cccount:
            wait("sp", ccsem, cccount, None)
        return dict(nops=len(ops), nwaits=nwaits, ecount=ecount, dcount=dcount)


class _Stop(Exception):
    pass


def build(debug=False, stage=None):
    nc = bass.Bass("TRN2", target_bir_lowering=False)
    P = Prog(nc)

    def ck(n):
        if stage is not None and n >= stage:
            raise _Stop()

    def din(name, shape, dt=F32):
        return nc.dram_tensor(name, list(shape), dt, kind="ExternalInput").ap()

    def dout(name, shape, dt=F32):
        return nc.dram_tensor(name, list(shape), dt, kind="ExternalOutput").ap()

    x_in = [din("xp", [512, 1024]), din("xs", [512, 1024])]
    cache_d = din("cache", [2, 256, 288])
    cond_d = din("cond", [2, 1024])
    wpk = din("wpk", [2, 128, PK_COLS])
    b_mod = din("b_mod", [2, 3072])
    g_pre = din("g_pre", [2, 1024])
    g_post = din("g_post", [2, 1024])
    conv_w = din("conv_w", [2, 3, 512])
    conv_b = din("conv_b", [2, 512])
    g_q = din("g_q", [2, 384])
    g_kv = din("g_kv", [2, 256])
    pool_scale = din("pool_scale", [2, 512])
    c_ident = din("c_ident", [128, 128])
    c_ropek = din("c_ropek", [128, 2, 4, 32])
    c_ropeq = din("c_ropeq", [96, 2, 512])
    c_poolp = din("c_poolp", [128, 16, 128])
    c_pools = din("c_pools", [128, 40, 128])
    c_halo = din("c_halo", [72, 8, 128])
    c_selc = din("c_selc", [72, 2])
    y_out = [dout("yp", [512, 1024]), dout("ys", [512, 1024])]
    lat_o = dout("lat", [2, 512, 288])
    modvec = din("modvec", [2, 1280])
    agm_in = [nc.dram_tensor("agm_in%d" % l, [2, 768], F32).ap() for l in range(2)]
    agm_out = [nc.dram_tensor("agm_out%d" % l, [8, 768], F32).ap() for l in range(2)]
    aginA = [nc.dram_tensor("aginA%d" % l, [288, 512], BF16).ap() for l in range(2)]
    agoutA = [nc.dram_tensor("agoutA%d" % l, [4 * 288, 512], BF16).ap() for l in range(2)]
    aginB = [nc.dram_tensor("aginB%d" % l, [18, 512], BF16).ap() for l in range(2)]
    agoutB = [nc.dram_tensor("agoutB%d" % l, [4 * 18, 512], BF16).ap() for l in range(2)]

    dbg = {}

    def sb(name, shape, dt=F32):
        return nc.alloc_sbuf_tensor("sb_" + name, list(shape), dt)

    X = sb("X", [128, 4, 1024])
    xn = [sb("xn%d" % i, [128, 1024], BF16) for i in range(2)]
    junk = sb("junk", [128, 1024], BF16)
    hnT = sb("hnT", [128, 8, 512], BF16)
    mrgT = sb("mrgT", [128, 8, 512], BF16)
    NWB = 4
    WB = [sb("WB%d" % i, [128, 4096], BF16) for i in range(NWB)]
    WB8 = [w[:, :].rearrange("p (k w) -> p k w", k=8) for w in WB]
    wbr = [sb("wbr%d" % i, [128, 12, 128], BF16) for i in range(2)]
    wq = sb("wq", [128, 3, 768], BF16)
    wqs = sb("wqs", [128, 3, 768], BF16)
    wkv = sb("wkv", [128, 2, 1024], BF16)
    wpool = sb("wpool", [128, 4, 128], BF16)
    thg = [sb("thg%d" % i, [128, 512], BF16) for i in range(3)]
    ga = sb("ga", [128, 4, 512], BF16)
    gb = sb("gb", [128, 4, 512], BF16)
    gc = sb("gc", [128, 4, 512], BF16)
    tht = [sb("tht%d" % i, [128, 512], BF16) for i in range(2)]
    ARW = 2304
    arena = sb("arena", [128, 3 * ARW], BF16)
    acT = arena[:, 0:2048].rearrange("p (c t) -> p c t", c=4)
    cu_tm = arena[:, ARW:ARW + 2048].rearrange("p (c t) -> p c t", c=4)
    uT = arena[:, 2 * ARW:2 * ARW + 2064].rearrange("p (c t) -> p c t", c=4)
    KT = [arena[0:96, 0:ARW], arena[0:96, ARW:2 * ARW]]
    PT = [arena[:, 2 * ARW + i * 512:2 * ARW + (i + 1) * 512] for i in range(3)]
    pooledT = [sb("pooledT%d" % i, [128, 512], BF16) for i in range(2)]
    lat = sb("lat", [128, 4, 288])
    lat_bf = sb("lat_bf", [128, 4, 288], BF16)
    qn = sb("qn", [128, 4, 384], BF16)
    qnT = sb("qnT", [128, 3, 512], BF16)
    ckvT = sb("ckvT", [128, 2, NK_S], BF16)
    KR = sb("KR", [128, NK_S], BF16)
    H = sb("H", [72, 512], BF16)
    Vh = [sb("Vh0", [128, 18, 65], BF16), sb("Vh1", [128, 18, 128], BF16)]
    rcp1 = sb("rcp", [128, 512])
    rcp = [rcp1, rcp1]
    onesel = sb("onesel", [128, 128])
    QT = [sb("QT%d" % i, [96, 512], BF16) for i in range(2)]
    qtmp = [sb("qtmp%d" % i, [96, 512]) for i in range(2)]
    tn = [sb("tn%d" % i, [128, 512]) for i in range(2)]
    ttmp = [sb("ttmp%d" % i, [128, 512]) for i in range(2)]
    mrow = [qtmp[i][0:2, :] for i in range(2)]
    mbc = [tn[i][0:2, :] for i in range(2)]
    mbc2 = [ttmp[i][0:2, :] for i in range(2)]
    GG = sb("GG", [128, 1024])
    G1 = sb("G1", [128, 8])
    SH = sb("SH", [128, 8])
    gq_bc = sb("gq_bc", [128, 384])
    gkv_bc = sb("gkv_bc", [128, 256])
    cwT = sb("cwT", [128, 3, 4])
    cbT = sb("cbT", [128, 4])
    psT = sb("psT", [128, 4])
    Dg = sb("Dg", [128, 12, 128], BF16)
    ident = sb("ident", [128, 128], BF16)
    ropek = sb("ropek", [128, 2, 4, 32])
    ropeq = sb("ropeq", [96, 2, 512])
    rk1 = sb("rk1", [128, 4, 32])
    rk2 = sb("rk2", [128, 4, 32])
    poolp = sb("poolp", [128, 16, 128], BF16)
    pools = sb("pools", [128, 40, 128], BF16)
    halo = sb("halo", [72, 8, 128], BF16)
    selc = sb("selc", [72, 2], BF16)
    cache_bf = lat_bf[:, 0:2, :]
    condT = sb("condT", [128, 8, 2])
    condth = sb("condth", [128, 8, 2])
    condS = sb("condS", [128, 8, 2], BF16)
    uh = sb("uh", [2, 512], BF16)
    uha = ttmp[1][0:2, :]
    st_ss = sb("st_ss", [128, 4])
    st_a = sb("st_a", [128, 4])
    st_y = sb("st_y", [128, 4])
    negh = sb("negh", [128, 4])
    ps = [nc.alloc_psum_tensor("ps%d" % i, [128, 512], F32) for i in range(8)]
    psb = [p_[:].bitcast(BF16) for p_ in ps]

    outs = []

    def dump(name, ap, shape, keys, dt=F32):
        if not debug:
            return
        d = dout("dbg_" + name, shape, dt)
        dbg[name] = d
        outs.append(P.dma("sp", lambda e: e.dma_start(out=d, in_=ap), reads=keys))

    wb_next = [0]
    wb_busy = [False] * NWB

    def wb_alloc():
        for _ in range(NWB):
            b = wb_next[0] % NWB
            wb_next[0] += 1
            if not wb_busy[b]:
                wb_busy[b] = True
                return b
        raise RuntimeError("no free weight buffer")

    def wb_release(b):
        wb_busy[b] = False

    rot = {}

    def rotate(name, banks):
        i = rot.get(name, 0)
        rot[name] = i + 1
        return banks[i % len(banks)]

    def pk(l, name):
        o, n = PK_OFF[name]
        return wpk[l, :, o:o + n]

    def rsqrt(y_ap, a_ap, k, rkeys, wkey, iters=3):
        P.op("pool", lambda e: e.tensor_tensor(out=y_ap, in0=a_ap, in1=negh[:, 0:k], op=ALU.pow),
             reads=list(rkeys) + ["negh"], writes=[wkey])

    HK = ["hn%d" % k for k in range(8)]
    ARK = ["ar_a", "ar_b", "ar_c0", "ar_c1", "ar_c2"]

    def emit_consts():
        P.dma("pool", lambda e: e.dma_start(out=ident[:], in_=c_ident), writes=["ident"])
        P.dma("sp", lambda e: e.dma_start(out=ropek[:], in_=c_ropek), writes=["ropek"])
        P.dma("sp", lambda e: e.dma_start(out=ropeq[:], in_=c_ropeq), writes=["ropeq"])
        P.dve(lambda e: e.memset(negh[:], -0.5), writes=["negh"])
        P.dve(lambda e: e.memset(Vh[0][:], 1.0), writes=["Vh0"])
        P.dve(lambda e: e.memset(Vh[1][:], 0.0), writes=["Vh1"])
        P.dve(lambda e: e.memset(Vh[1][:, :, 0:1], 1.0), writes=["Vh1"])
        P.dve(lambda e: e.memset(onesel[:], 0.0), writes=["onesel"])
        P.dve(lambda e: e.memset(onesel[64:65, 0:64], 1.0), writes=["onesel"])
        P.dve(lambda e: e.memset(onesel[0:1, 64:128], 1.0), writes=["onesel"])
        P.dve(lambda e: e.memset(rcp1[:], 1.0), writes=["rcp0", "rcp1"])
        for ci in range(2):
            P.dma("sp", lambda e, ci=ci: e.dma_start(out=condT[:, :, ci], in_=cond_d[ci].rearrange("(k p) -> p k", p=128),
                                                     allow_slow_non_contiguous=True), writes=["condT"])
        P.act(lambda e: e.activation(out=condth[:], in_=condT[:], func=AF.Tanh, scale=0.5),
              reads=["condT"], writes=["condth"])
        P.dve(lambda e: e.scalar_tensor_tensor(out=condth[:], in0=condth[:], scalar=1.0, in1=condT[:],
                                               op0=ALU.add, op1=ALU.mult), reads=["condth", "condT"], writes=["condth"])
        P.dve(lambda e: e.tensor_scalar(out=condS[:], in0=condth[:], scalar1=0.5, scalar2=None, op0=ALU.mult),
              reads=["condth"], writes=["condS"])

    def emit_mod(l, blks=None):
        bA = wb_alloc()
        P.dma("pool", lambda e: e.dma_start(out=WB[bA][:, :], in_=pk(l, "modA")), writes=["WB%d" % bA])
        bB = wb_alloc()
        P.dma("pool", lambda e: e.dma_start(out=WB[bB][:, 0:2048], in_=pk(l, "modB")), writes=["WB%d" % bB])
        vB = WB[bB][:, 0:2048].rearrange("p (k w) -> p k w", k=8)
        P.dma("sp", lambda e: e.dma_start(out=mbc[0], in_=modvec[l, 0:512].partition_broadcast(2)), writes=["tn0"])
        P.dma("sp", lambda e: e.dma_start(out=mbc[1][:, 0:256], in_=modvec[l, 512:768].partition_broadcast(2)), writes=["tn1"])
        P.dma("sp", lambda e: e.dma_start(out=mbc2[0][:, 0:256], in_=modvec[l, 768:1024].partition_broadcast(2)), writes=["ttmp0"])
        P.dma("sp", lambda e: e.dma_start(out=mbc2[1][:, 0:256], in_=modvec[l, 1024:1280].partition_broadcast(2)), writes=["ttmp1"])
        for kc in range(8):
            P.pe(lambda e, kc=kc: e.matmul(ps[0][0:2, :], condS[:, kc, :], WB8[bA][:, kc, :], start=(kc == 0), stop=(kc == 7)),
                 reads=["condS", "WB%d" % bA], writes=["ps0"])
        for kc in range(8):
            P.pe(lambda e, kc=kc: e.matmul(ps[1][0:2, 0:256], condS[:, kc, :], vB[:, kc, :], start=(kc == 0), stop=(kc == 7)),
                 reads=["condS", "WB%d" % bB], writes=["ps1"])
        wb_release(bA)
        wb_release(bB)
        P.dve(lambda e: e.tensor_tensor(out=mrow[0], in0=ps[0][0:2, :], in1=mbc[0], op=ALU.add),
              reads=["ps0", "tn0"], writes=["qtmp0"])
        P.dve(lambda e: e.scalar_tensor_tensor(out=mrow[0][:, 256:512], in0=mrow[0][:, 256:512], scalar=1.0, in1=mbc2[0][:, 0:256],
                                               op0=ALU.add, op1=ALU.mult), reads=["qtmp0", "ttmp0"], writes=["qtmp0"])
        P.dve(lambda e: e.tensor_tensor(out=mrow[1][:, 0:256], in0=ps[1][0:2, 0:256], in1=mbc[1][:, 0:256], op=ALU.add),
              reads=["ps1", "tn1"], writes=["qtmp1"])
        P.dve(lambda e: e.tensor_tensor(out=mrow[1][:, 0:256], in0=mrow[1][:, 0:256], in1=mbc2[1][:, 0:256], op=ALU.mult),
              reads=["qtmp1", "ttmp1"], writes=["qtmp1"])
        P.dma("sp", lambda e: e.dma_start(out=agm_in[l][:, 0:512], in_=mrow[0]), reads=["qtmp0"], writes=["agm_in%da" % l])
        P.dma("sp", lambda e: e.dma_start(out=agm_in[l][:, 512:768], in_=mrow[1][:, 0:256]), reads=["qtmp1"], writes=["agm_in%db" % l])
        P.cc(lambda e: e.collective_compute("AllGather", ALU.bypass, replica_groups=[[0, 1, 2, 3], [4, 5, 6, 7]],
                                            ins=[agm_in[l]], outs=[agm_out[l]]),
             reads=["agm_in%da" % l, "agm_in%db" % l], writes=["agm_out%d" % l])

    def modsrc(l, cond, t):
        return agm_out[l].rearrange("(r c) (t h p) -> r c t h p", c=2, t=3, h=2)[:, cond, t, :, :]

    def load_layer_small(l, cond, part="ab"):
        if "a" in part:
          for h in range(2):
            P.dma("sp", lambda e, h=h: e.dma_start(out=SH[:, h:8:2], in_=modsrc(l, cond, 0)[:, h, :].rearrange("r p -> p r"),
                                                   allow_slow_non_contiguous=True), reads=["agm_out%d" % l], writes=["SH"])
            P.dma("sp", lambda e, h=h: e.dma_start(out=G1[:, h:8:2], in_=modsrc(l, cond, 1)[:, h, :].rearrange("r p -> p r"),
                                                   allow_slow_non_contiguous=True), reads=["agm_out%d" % l], writes=["G1"])
        if "b" not in part:
            return
        P.dma("sp", lambda e: e.dma_start(out=gq_bc[:], in_=g_q[l].partition_broadcast(128)), writes=["gq_bc"])
        P.dma("sp", lambda e: e.dma_start(out=gkv_bc[:], in_=g_kv[l].partition_broadcast(128)), writes=["gkv_bc"])
        for k in range(3):
            P.dma("sp", lambda e, k=k: e.dma_start(out=cwT[:, k, :], in_=conv_w[l, k].rearrange("(c p) -> p c", p=128),
                                                   allow_slow_non_contiguous=True), writes=["cwT"])
        P.dma("sp", lambda e: e.dma_start(out=cbT[:], in_=conv_b[l].rearrange("(c p) -> p c", p=128),
                                          allow_slow_non_contiguous=True), writes=["cbT"])
        P.dma("sp", lambda e: e.dma_start(out=psT[:], in_=pool_scale[l].rearrange("(c p) -> p c", p=128),
                                          allow_slow_non_contiguous=True), writes=["psT"])
        for cc in range(4):
            for k in range(3):
                P.dve(lambda e, cc=cc, k=k: e.tensor_scalar(out=Dg[:, cc * 3 + k, :], in0=ident[:],
                                                            scalar1=cwT[:, k, cc:cc + 1], scalar2=None, op0=ALU.mult),
                      reads=["ident", "cwT"], writes=["Dg"])

    pn_ss = sb("pn_ss", [128, 4])
    pn_a = sb("pn_a", [128, 4])
    pn_y = sb("pn_y", [128, 4])

    def prenorm_a(tt, bank):
        c = slice(tt, tt + 1)
        P.act(lambda e: e.activation(out=junk[:], in_=X[:, tt, :], func=AF.Square, accum_out=pn_ss[:, c]),
              reads=["X%d" % tt], writes=["junk", "pss%d" % tt])
        P.dve(lambda e: e.tensor_scalar(out=pn_a[:, c], in0=pn_ss[:, c], scalar1=1.0 / 1024, scalar2=EPS,
                                        op0=ALU.mult, op1=ALU.add), reads=["pss%d" % tt], writes=["pa%d" % tt])
        rsqrt(pn_y[:, c], pn_a[:, c], 1, ["pa%d" % tt], "py%d" % tt)
        s = tt % 2
        P.act(lambda e: e.activation(out=xn[s][:], in_=X[:, tt, :], func=AF.Identity, scale=pn_y[:, c]),
              reads=["X%d" % tt, "py%d" % tt], writes=["xn%d" % s])
        for kc in range(8):
            P.pe(lambda e, kc=kc: e.transpose(psb[bank][:, kc * 128:(kc + 1) * 128], xn[s][:, kc * 128:(kc + 1) * 128], ident[:]),
                 reads=["xn%d" % s, "ident"], writes=["ps%d" % bank])

    def prenorm_b(tt, bank):
        for kc in range(8):
            dst = hnT[:, kc, tt * 128:(tt + 1) * 128]
            src = psb[bank][:, kc * 128:(kc + 1) * 128]
            if tt % 2 == 0:
                P.act(lambda e, dst=dst, src=src, kc=kc: e.activation(out=dst, in_=src, func=AF.Identity,
                                                                      scale=G1[:, kc:kc + 1], bias=SH[:, kc:kc + 1]),
                      reads=["ps%d" % bank, "G1", "SH"], writes=[HK[kc] + "_%d" % tt])
            else:
                P.dve(lambda e, dst=dst, src=src, kc=kc: e.tensor_scalar(out=dst, in0=src, scalar1=G1[:, kc:kc + 1],
                                                                         scalar2=SH[:, kc:kc + 1], op0=ALU.mult, op1=ALU.add),
                      reads=["ps%d" % bank, "G1", "SH"], writes=[HK[kc] + "_%d" % tt])

    def prenorm_tile(l, g, tt):
        bank = rotate("tr", [6, 7])
        prenorm_a(tt, bank)
        prenorm_b(tt, bank)

    wname = {c0: nm for nm, c0, W in W_BLOCKS}
    wview = {}

    def load_win(l, c0, W):
        b = wb_alloc()
        P.dma("pool", lambda e: e.dma_start(out=WB[b][:, 0:8 * W], in_=pk(l, "win_" + wname[c0])), writes=["WB%d" % b])
        wview[b] = WB[b][:, 0:8 * W].rearrange("p (k w) -> p k w", k=8)
        return b

    consts_loaded = []

    def group_layer(l, g, b_kv_pre=None, nxt=None):
        samp = g == 1
        HKall = [[HK[kc] + "_%d" % tt for tt in range(4)] for kc in range(8)]
        if l == 0:
            dump("hnT%d" % g, hnT[:], [128, 8, 512], [k for ks in HKall for k in ks], BF16)
        ck(1)

        def fm_block(b, consumer):
            for cc in range(4):
                bank = rotate("fm", [0, 1, 2, 3])
                for kc in range(8):
                    P.pe(lambda e, cc=cc, kc=kc, bank=bank: e.matmul(ps[bank][:], wview[b][:, kc, cc * 128:(cc + 1) * 128],
                                                                     hnT[:, kc, :], start=(kc == 0), stop=(kc == 7)),
                         reads=["WB%d" % b] + HKall[kc], writes=["ps%d" % bank])
                consumer(cc, bank)

        def tm_block(b, W, consumer):
            for tt in range(4):
                bank = rotate("tm", [4, 5])
                for kc in range(8):
                    P.pe(lambda e, tt=tt, kc=kc, bank=bank: e.matmul(ps[bank][:, 0:W], hnT[:, kc, tt * 128:(tt + 1) * 128],
                                                                     wview[b][:, kc, 0:W], start=(kc == 0), stop=(kc == 7)),
                         reads=["WB%d" % b, HK[kc] + "_%d" % tt], writes=["ps%d" % bank])
                consumer(tt, bank)

        b_kv = b_kv_pre if b_kv_pre is not None else load_win(l, OFF_KV, 288)
        b_cu = load_win(l, OFF_CU, 512)
        b_ac = load_win(l, OFF_AC, 512)

        def kv_cons(tt, bank):
            P.act(lambda e: e.activation(out=lat[:, tt, :], in_=ps[bank][:, 0:288], func=AF.Copy),
                  reads=["ps%d" % bank], writes=["lat%d" % tt])
            P.act(lambda e: e.activation(out=junk[:, 0:256], in_=ps[bank][:, 0:256], func=AF.Square,
                                         accum_out=st_ss[:, tt:tt + 1]), reads=["ps%d" % bank], writes=["junk", "ss%d" % tt])
        tm_block(b_kv, 288, kv_cons)
        wb_release(b_kv)
        P.dve(lambda e: e.tensor_scalar(out=st_a[:], in0=st_ss[:], scalar1=1.0 / 256, scalar2=EPS,
                                        op0=ALU.mult, op1=ALU.add), reads=["ss%d" % t for t in range(4)], writes=["st_a"])
        rsqrt(st_y[:], st_a[:], 4, ["st_a"], "st_y")
        for tt in range(4):
            P.dve(lambda e, tt=tt: e.scalar_tensor_tensor(out=lat[:, tt, 0:256], in0=lat[:, tt, 0:256],
                                                          scalar=st_y[:, tt:tt + 1], in1=gkv_bc[:],
                                                          op0=ALU.mult, op1=ALU.mult),
                  reads=["lat%d" % tt, "st_y", "gkv_bc"], writes=["lat%d" % tt])
        LK = ["lat%d" % t for t in range(4)]
        if samp:
            kr5 = lat[:, :, 256:288].rearrange("p t (a j i) -> p t a j i", a=2, j=2)
            r15 = rk1[:].rearrange("p t (a j i) -> p t a j i", a=2, j=2)
            r25 = rk2[:].rearrange("p t (a j i) -> p t a j i", a=2, j=2)
            sn5 = ropek[:, 1, :, :].rearrange("p t (a j i) -> p t a j i", a=2, j=2)
            P.dve(lambda e: e.tensor_tensor(out=rk1[:], in0=lat[:, :, 256:288], in1=ropek[:, 0, :, :], op=ALU.mult),
                  reads=LK + ["ropek"], writes=["rk1"])
            P.dve(lambda e: e.tensor_tensor(out=r25[:, :, :, 0, :], in0=kr5[:, :, :, 1, :], in1=sn5[:, :, :, 0, :], op=ALU.mult),
                  reads=LK + ["ropek"], writes=["rk2a"])
            P.dve(lambda e: e.tensor_tensor(out=r25[:, :, :, 1, :], in0=kr5[:, :, :, 0, :], in1=sn5[:, :, :, 1, :], op=ALU.mult),
                  reads=LK + ["ropek"], writes=["rk2b"])
            P.dve(lambda e: e.tensor_tensor(out=lat[:, :, 256:288], in0=rk1[:], in1=rk2[:], op=ALU.add),
                  reads=["rk1", "rk2a", "rk2b"], writes=LK)
        else:
            outs.append(P.dma("sp", lambda e: e.dma_start(out=lat_o[l].rearrange("(t p) f -> p t f", p=128), in_=lat[:]),
                              reads=LK))
        if l == 0:
            dump("lat%d" % g, lat[:], [128, 4, 288], LK)
        ck(2)
        P.dve(lambda e: e.tensor_copy(out=lat_bf[:], in_=lat[:]), reads=LK, writes=["lat_bf"])
        oc = 1792 if samp else 0
        ock = ["ck%d" % (oc // 256), "ck%d" % (oc // 256 + 1)]
        okr = ["kr%d" % (oc // 256), "kr%d" % (oc // 256 + 1)]
        bank = rotate("tr", [6, 7])
        bank2 = rotate("tr", [6, 7])
        for tt in range(4):
            for ch in range(2):
                P.pe(lambda e, tt=tt, ch=ch: e.transpose(psb[bank][:, ch * 512 + tt * 128:ch * 512 + (tt + 1) * 128],
                                                         lat_bf[:, tt, ch * 128:(ch + 1) * 128], ident[:]),
                     reads=["lat_bf", "ident"], writes=["ps%d" % bank])
            P.pe(lambda e, tt=tt: e.transpose(psb[bank2][0:96, tt * 128:(tt + 1) * 128], lat_bf[:, tt, 192:288], ident[:]),
                 reads=["lat_bf", "ident"], writes=["ps%d" % bank2])
        P.act(lambda e: e.activation(out=ckvT[:, 0, oc:oc + 512], in_=psb[bank][:, 0:512], func=AF.Copy),
              reads=["ps%d" % bank], writes=[k + "_0" for k in ock])
        P.dve(lambda e: e.tensor_copy(out=ckvT[:, 1, oc:oc + 512], in_=psb[bank][:, 512:1024]),
              reads=["ps%d" % bank], writes=[k + "_1" for k in ock])
        P.act(lambda e: e.activation(out=KR[64:96, oc:oc + 512], in_=psb[bank2][64:96, 0:512], func=AF.Copy),
              reads=["ps%d" % bank2], writes=okr)
        if samp:
            P.dma("sp", lambda e: e.dma_start(out=aginA[l][0:256, :].rearrange("(c p) t -> p c t", p=128),
                                              in_=ckvT[:, :, oc:oc + 512]),
                  reads=[k + "_0" for k in ock] + [k + "_1" for k in ock], writes=["aginA%da" % l])
            P.dma("sp", lambda e: e.dma_start(out=aginA[l][256:288, :], in_=KR[64:96, oc:oc + 512]), reads=okr, writes=["aginA%db" % l])
            P.cc(lambda e: e.collective_compute("AllGather", ALU.bypass, replica_groups=[[0, 1, 2, 3], [4, 5, 6, 7]],
                                                ins=[aginA[l]], outs=[agoutA[l]]),
                 reads=["aginA%da" % l, "aginA%db" % l], writes=["agoutA%d" % l])

        P.dve(lambda e: e.memset(uT, 0.0), writes=["ar_c0", "ar_c1", "ar_c2"])

        def cu_cons(tt, bank):
            P.act(lambda e: e.activation(out=cu_tm[:, tt, :], in_=ps[bank][:], func=AF.Copy),
                  reads=["ps%d" % bank], writes=["ar_b"])
        tm_block(b_cu, 512, cu_cons)
        wb_release(b_cu)
        b_ax = load_win(l, OFF_AX, 512)

        def ac_cons(cc, bank):
            P.act(lambda e: e.activation(out=acT[:, cc, :], in_=ps[bank][:], func=AF.Copy),
                  reads=["ps%d" % bank], writes=["ar_a"])
        fm_block(b_ac, ac_cons)
        if samp:
            bank = rotate("tm", [4, 5])
            for kc in range(8):
                P.pe(lambda e, kc=kc, bank=bank: e.matmul(ps[bank][0:2, :], hnT[:, kc, 0:512:511], wview[b_ac][:, kc, :],
                                                          start=(kc == 0), stop=(kc == 7)),
                     reads=["WB%d" % b_ac] + HKall[kc], writes=["ps%d" % bank])
            P.act(lambda e, bank=bank: e.activation(out=uha, in_=ps[bank][0:2, :], func=AF.Copy),
                  reads=["ps%d" % bank], writes=["ttmp1"])
        wb_release(b_ac)
        b_q = load_win(l, OFF_Q, 384)

        def ax_cons(cc, bank):
            if samp:
                dst = uT[:, cc, 1:513]
                P.dve(lambda e: e.tensor_tensor(out=dst, in0=acT[:, cc, :], in1=ps[bank][:], op=ALU.mult),
                      reads=["ps%d" % bank, "ar_a"], writes=["ar_c0", "ar_c1", "ar_c2"])
            else:
                dst = uT[:, cc, 0:516].rearrange("p (s t) -> p s t", s=2)[:, :, 1:257]
                P.dve(lambda e: e.tensor_tensor(out=dst, in0=acT[:, cc, :].rearrange("p (s t) -> p s t", s=2),
                                                in1=ps[bank][:].rearrange("p (s t) -> p s t", s=2), op=ALU.mult),
                      reads=["ps%d" % bank, "ar_a"], writes=["ar_c0", "ar_c1", "ar_c2"])
        fm_block(b_ax, ax_cons)
        if samp:
            bank = rotate("tm", [4, 5])
            for kc in range(8):
                P.pe(lambda e, kc=kc, bank=bank: e.matmul(ps[bank][0:2, :], hnT[:, kc, 0:512:511], wview[b_ax][:, kc, :],
                                                          start=(kc == 0), stop=(kc == 7)),
                     reads=["WB%d" % b_ax] + HKall[kc], writes=["ps%d" % bank])
            P.dve(lambda e, bank=bank: e.tensor_tensor(out=uh[:], in0=uha, in1=ps[bank][0:2, :], op=ALU.mult),
                  reads=["ps%d" % bank, "ttmp1"], writes=["uh"])
        wb_release(b_ax)
        b_az = load_win(l, OFF_AZ, 512)
        ck(3)

        if samp:
            P.dma("sp", lambda e: e.dma_start(out=aginB[l][0:8, :], in_=cu_tm[0:8, 0, :]), reads=["ar_b"], writes=["aginB%da" % l])
            P.dma("sp", lambda e: e.dma_start(out=aginB[l][8:16, :], in_=cu_tm[120:128, 3, :]), reads=["ar_b"], writes=["aginB%db" % l])
            P.dma("sp", lambda e: e.dma_start(out=aginB[l][16:18, :], in_=uh[:]), reads=["uh"], writes=["aginB%dc" % l])
            P.cc(lambda e: e.collective_compute("AllGather", ALU.bypass, replica_groups=[[0, 1, 2, 3], [4, 5, 6, 7]],
                                                ins=[aginB[l]], outs=[agoutB[l]]),
                 reads=["aginB%d%s" % (l, x) for x in "abc"], writes=["agoutB%d" % l])
        def q_cons(tt, bank):
            P.act(lambda e: e.activation(out=junk[:, 0:384], in_=ps[bank][:, 0:384], func=AF.Square,
                                         accum_out=st_ss[:, tt:tt + 1]), reads=["ps%d" % bank], writes=["junk", "ss%d" % tt])
            P.dve(lambda e: e.tensor_copy(out=qn[:, tt, :], in_=ps[bank][:, 0:384]), reads=["ps%d" % bank], writes=["qn%d" % tt])
        tm_block(b_q, 384, q_cons)
        wb_release(b_q)
        b_ab = load_win(l, OFF_AB, 512)
        P.dve(lambda e: e.tensor_scalar(out=st_a[:], in0=st_ss[:], scalar1=1.0 / 384, scalar2=EPS,
                                        op0=ALU.mult, op1=ALU.add), reads=["ss%d" % t for t in range(4)], writes=["st_a"])
        rsqrt(st_y[:], st_a[:], 4, ["st_a"], "st_y")
        for tt in range(4):
            P.dve(lambda e, tt=tt: e.scalar_tensor_tensor(out=qn[:, tt, :], in0=qn[:, tt, :], scalar=st_y[:, tt:tt + 1],
                                                          in1=gq_bc[:], op0=ALU.mult, op1=ALU.mult),
                  reads=["qn%d" % tt, "st_y", "gq_bc"], writes=["qn%d" % tt])
        bank = rotate("tr", [6, 7])
        bank2 = rotate("tr", [6, 7])
        for tt in range(4):
            for ch in range(3):
                bk = bank if ch < 2 else bank2
                co = (ch % 2) * 512 + tt * 128
                P.pe(lambda e, tt=tt, ch=ch, bk=bk, co=co: e.transpose(psb[bk][:, co:co + 128], qn[:, tt, ch * 128:(ch + 1) * 128], ident[:]),
                     reads=["qn%d" % tt, "ident"], writes=["ps%d" % bk])
        P.act(lambda e: e.activation(out=qnT[:, 0, :], in_=psb[bank][:, 0:512], func=AF.Copy), reads=["ps%d" % bank], writes=["qnT0"])
        P.dve(lambda e: e.tensor_copy(out=qnT[:, 1, :], in_=psb[bank][:, 512:1024]), reads=["ps%d" % bank], writes=["qnT1"])
        P.act(lambda e: e.activation(out=qnT[:, 2, :], in_=psb[bank2][:, 0:512], func=AF.Copy), reads=["ps%d" % bank2], writes=["qnT2"])

        ck(4)
        def az_cons(cc, bank):
            s = rotate("tht", [0, 1])
            P.act(lambda e: e.activation(out=tht[s][:], in_=ps[bank][:], func=AF.Tanh, scale=0.5),
                  reads=["ps%d" % bank], writes=["tht%d" % s])
            P.dve(lambda e: e.scalar_tensor_tensor(out=ga[:, cc, :], in0=tht[s][:], scalar=1.0, in1=ps[bank][:],
                                                   op0=ALU.add, op1=ALU.mult),
                  reads=["tht%d" % s, "ps%d" % bank], writes=["ga%d" % cc])
        fm_block(b_az, az_cons)
        wb_release(b_az)
        b_bz = load_win(l, OFF_BZ, 512)

        def ab_cons(cc, bank):
            P.dve(lambda e: e.tensor_tensor(out=ga[:, cc, :], in0=ga[:, cc, :], in1=ps[bank][:], op=ALU.mult),
                  reads=["ga%d" % cc, "ps%d" % bank], writes=["ga%d" % cc])
        fm_block(b_ab, ab_cons)
        wb_release(b_ab)
        b_cz = load_win(l, OFF_CZ, 512)

        def gate_cons(dst, key):
            def cons(cc, bank):
                s = rotate("tht", [0, 1])
                P.act(lambda e: e.activation(out=tht[s][:], in_=ps[bank][:], func=AF.Tanh, scale=0.5),
                      reads=["ps%d" % bank], writes=["tht%d" % s])
                P.dve(lambda e: e.scalar_tensor_tensor(out=dst[:, cc, :], in0=tht[s][:], scalar=1.0, in1=ps[bank][:],
                                                       op0=ALU.add, op1=ALU.mult),
                      reads=["tht%d" % s, "ps%d" % bank], writes=[key + "%d" % cc])
            return cons
        fm_block(b_bz, gate_cons(gb, "gb"))
        wb_release(b_bz)
        fm_block(b_cz, gate_cons(gc, "gc"))
        wb_release(b_cz)

        ck(5)
        if not consts_loaded:
            consts_loaded.append(True)
            P.dma("pool", lambda e: e.dma_start(out=poolp[:], in_=c_poolp), writes=["poolp"])
            P.dma("pool", lambda e: e.dma_start(out=pools[:], in_=c_pools), writes=["pools"])
            P.dma("pool", lambda e: e.dma_start(out=halo[:], in_=c_halo), writes=["halo"])
            P.dma("pool", lambda e: e.dma_start(out=selc[:], in_=c_selc), writes=["selc"])
        P.dma("pool", lambda e: e.dma_start(out=wq[:].rearrange("p c w -> p (c w)"), in_=pk(l, "wuq")), writes=["wq"])
        P.dma("pool", lambda e: e.dma_start(out=wkv[:].rearrange("p c w -> p (c w)"), in_=pk(l, "wukv")), writes=["wkv"])
        P.dma("pool", lambda e: e.dma_start(out=wpool[:].rearrange("p g d -> p (g d)"), in_=pk(l, "poolw")), writes=["wpool"])
        if samp:
            P.act(lambda e: e.activation(out=wqs[:], in_=wq[:], func=AF.Copy), reads=["wq"], writes=["wqs"])
            v6 = lambda t: t[:].rearrange("p c (h d) -> p (c h) d", d=96)[:, :, 64:96].rearrange("p n (a j i) -> p n a j i", a=2, j=2)
            P.dve(lambda e: e.tensor_copy(out=v6(wqs)[:, :, :, 0, :], in_=v6(wq)[:, :, :, 1, :]), reads=["wq", "wqs"], writes=["wqs"])
            P.dve(lambda e: e.tensor_copy(out=v6(wqs)[:, :, :, 1, :], in_=v6(wq)[:, :, :, 0, :]), reads=["wq", "wqs"], writes=["wqs"])

        if samp:
            ak = "agoutA%d" % l
            for ch in range(2):
                src = agoutA[l].rearrange("(r f) t -> f r t", r=4)[ch * 128:(ch + 1) * 128]
                P.dma("sp", lambda e, ch=ch, src=src: e.dma_start(
                    out=ckvT[:, ch, 256:NK_S].rearrange("p (r t) -> p r t", r=4), in_=src),
                    reads=[ak], writes=["ck%d_%d" % (i, ch) for i in range(1, 9)])
            src = agoutA[l].rearrange("(r f) t -> f r t", r=4)[256:288]
            P.dma("sp", lambda e: e.dma_start(out=KR[64:96, 256:NK_S].rearrange("p (r t) -> p r t", r=4), in_=src),
                  reads=[ak], writes=["kr%d" % i for i in range(1, 9)])
            P.dma("sp", lambda e: e.dma_start(out=H[:, :], in_=agoutB[l]), reads=["agoutB%d" % l], writes=["H"])
            bank = rotate("misc", [4, 5])
            for cc in range(4):
                P.pe(lambda e, cc=cc, bank=bank: e.matmul(ps[bank][:, cc * 2:cc * 2 + 2], H[:, cc * 128:(cc + 1) * 128], selc[:],
                                                          start=(cc == 0), stop=True, skip_group_check=True),
                     reads=["H", "selc"], writes=["ps%d" % bank])
            P.dve(lambda e, bank=bank: e.tensor_copy(out=uT[:, :, 0:514:513],
                                                     in_=ps[bank][:, 0:8].rearrange("p (c s) -> p c s", c=4)),
                  reads=["ps%d" % bank], writes=["ar_c0", "ar_c1", "ar_c2"])

        ck(6)
        if g == 0 and l == 0:
            emit_mod(1)
        for cc in range(4):
            bank = rotate("cv", [0, 1])
            segs = [(0, 512, 0)] if samp else [(0, 256, 0), (256, 256, 258)]
            first = True
            for (t0, n, e0) in segs:
                for k in range(3):
                    P.pe(lambda e, cc=cc, k=k, t0=t0, n=n, e0=e0, bank=bank, first=first: e.matmul(
                        ps[bank][:, t0:t0 + n], Dg[:, cc * 3 + k, :], uT[:, cc, e0 + k:e0 + k + n],
                        start=first, stop=(k == 2), skip_group_check=True),
                        reads=["Dg", "ar_c0", "ar_c1", "ar_c2"], writes=["ps%d" % bank])
                    first = False
            P.dve(lambda e, cc=cc, bank=bank: e.scalar_tensor_tensor(out=ga[:, cc, :], in0=ps[bank][:], scalar=cbT[:, cc:cc + 1],
                                                                     in1=ga[:, cc, :], op0=ALU.add, op1=ALU.mult),
                  reads=["ps%d" % bank, "cbT", "ga%d" % cc], writes=["ga%d" % cc])
        if l == 0:
            dump("ya%d" % g, ga[:], [128, 4, 512], ["ga%d" % c for c in range(4)], BF16)

        ck(7)
        pb = pools if samp else poolp
        pbk = "pools" if samp else "poolp"
        if samp:
            pairs = [(i, j) for j in range(4) for i in range(4) if abs(i - j) <= 1]
        else:
            pairs = [(0, 0), (1, 0), (0, 1), (1, 1), (2, 2), (3, 2), (2, 3), (3, 3)]
        for gi in range(4):
            bank = rotate("pl", [2, 3])
            first = True
            for j in range(4):
                terms = []
                for (i, jj) in pairs:
                    if jj != j:
                        continue
                    if samp:
                        idx = gi * 10 + [p_ for p_ in pairs].index((i, j))
                    else:
                        idx = gi * 4 + [(0, 0), (1, 0), (0, 1), (1, 1)].index((i % 2, j % 2))
                    terms.append((cu_tm[:, i, gi * 128:(gi + 1) * 128], pb[:, idx, :], ["ar_b", pbk]))
                if samp and j in (0, 3):
                    terms.append((H[:, gi * 128:(gi + 1) * 128], halo[:, gi * 2 + (0 if j == 0 else 1), :], ["H", "halo"]))
                for ti, (lh, rh, rk) in enumerate(terms):
                    P.pe(lambda e, lh=lh, rh=rh, j=j, bank=bank, first=first, lastt=(ti == len(terms) - 1): e.matmul(
                        ps[bank][:, j * 128:(j + 1) * 128], lh, rh, start=first, stop=lastt, skip_group_check=True),
                        reads=rk, writes=["ps%d" % bank])
                    first = False
            s = gi % 2
            P.act(lambda e, s=s, bank=bank: e.activation(out=pooledT[s][:], in_=ps[bank][:], func=AF.Copy),
                  reads=["ps%d" % bank], writes=["pooledT%d" % s])
            bank2 = rotate("mx", [4, 5])
            P.pe(lambda e, gi=gi, s=s, bank2=bank2: e.matmul(ps[bank2][:], wpool[:, gi, :], pooledT[s][:], start=True, stop=True),
                 reads=["wpool", "pooledT%d" % s], writes=["ps%d" % bank2])
            P.dve(lambda e, gi=gi, bank2=bank2: e.scalar_tensor_tensor(out=gc[:, gi, :], in0=ps[bank2][:], scalar=psT[:, gi:gi + 1],
                                                                       in1=gc[:, gi, :], op0=ALU.mult, op1=ALU.mult),
                  reads=["ps%d" % bank2, "psT", "gc%d" % gi], writes=["gc%d" % gi])
        if l == 0:
            dump("yc%d" % g, gc[:], [128, 4, 512], ["gc%d" % c for c in range(4)], BF16)

        ck(8)
        if samp:
            probs = [(0, 512, 0, NK_S)]
            nk_tot = NK_S
        else:
            probs = [(0, 256, 0, 256), (256, 256, 256, 256)]
            nk_tot = 512
        nkb = (nk_tot + 255) // 256
        CKR = lambda ch: ["ck%d_%d" % (i, ch) for i in range(nkb)]
        KRR = ["kr%d" % i for i in range(nkb)]
        nkt = nk_tot // 128

        def head_prep(h):
            s = h % 2
            ktk = ["ar_a", "ar_b"][s]
            steps = []
            c0 = 0
            while c0 < nk_tot:
                n = min(512, nk_tot - c0)

                def kstep(c0=c0, n=n):
                    bank = rotate("ku", [5])
                    for ch in range(2):
                        P.pe(lambda e, ch=ch: e.matmul(ps[bank][0:64, 0:n], wkv[:, ch, h * 128:h * 128 + 64],
                                                       ckvT[:, ch, c0:c0 + n], start=(ch == 0), stop=(ch == 1)),
                             reads=["wkv"] + CKR(ch), writes=["ps%d" % bank])
                    if samp:
                        P.dve(lambda e: e.tensor_copy(out=KT[s][0:64, c0:c0 + n], in_=ps[bank][0:64, 0:n]),
                              reads=["ps%d" % bank], writes=[ktk])
                    else:
                        P.act(lambda e: e.activation(out=KT[s][0:64, c0:c0 + n], in_=ps[bank][0:64, 0:n], func=AF.Copy),
                              reads=["ps%d" % bank], writes=[ktk])
                steps.append(kstep)
                c0 += n

            def rstep():
                P.dve(lambda e: e.tensor_copy(out=KT[s][64:96, 0:nk_tot], in_=KR[64:96, 0:nk_tot]), reads=KRR, writes=[ktk])
            steps.append(rstep)
            vo = 0 if s == 0 else 64
            vbank = {}
            t0 = 0
            while t0 < nkt:
                n = min(8, nkt - t0)
                for half in range(2):
                    def vstep(t0=t0, n=n, half=half):
                        if half == 0:
                            vbank[t0] = rotate("vu", [6])
                        bank = vbank[t0]
                        lo, hi = (0, (n + 1) // 2) if half == 0 else ((n + 1) // 2, n)
                        for r in range(lo, hi):
                            for ch in range(2):
                                P.pe(lambda e, r=r, ch=ch: e.matmul(
                                    ps[bank][:, r * 64:(r + 1) * 64], ckvT[:, ch, (t0 + r) * 128:(t0 + r + 1) * 128],
                                    wkv[:, ch, h * 128 + 64:h * 128 + 128], start=(r == 0 and ch == 0), stop=(ch == 1),
                                    skip_group_check=True),
                                    reads=["wkv"] + CKR(ch), writes=["ps%d" % bank])
                        if half == 1:
                            P.dve(lambda e: e.tensor_copy(
                                out=Vh[s][:, t0:t0 + n, vo:vo + 64], in_=ps[bank][:, 0:n * 64].rearrange("p (r d) -> p r d", d=64)),
                                reads=["ps%d" % bank], writes=["Vh%d" % s])
                    steps.append(vstep)
                t0 += n

            def qstep():
                bank = rotate("qu", [7])
                for ch in range(3):
                    P.pe(lambda e, ch=ch: e.matmul(ps[bank][0:96, :], wq[:, ch, h * 96:(h + 1) * 96], qnT[:, ch, :],
                                                   start=(ch == 0), stop=(ch == 2)),
                         reads=["wq", "qnT%d" % ch], writes=["ps%d" % bank])
                if samp:
                    P.dve(lambda e: e.tensor_tensor(out=qtmp[0][:], in0=ps[bank][0:96, :], in1=ropeq[:, 0, :], op=ALU.mult),
                          reads=["ps%d" % bank, "ropeq"], writes=["qtmp0"])
                else:
                    P.act(lambda e: e.activation(out=QT[s][:], in_=ps[bank][0:96, :], func=AF.Copy),
                          reads=["ps%d" % bank], writes=["QT%d" % s])
            steps.append(qstep)
            if samp:
                def qstep2():
                    bank2 = rotate("qu2", [5])
                    for ch in range(3):
                        P.pe(lambda e, ch=ch: e.matmul(ps[bank2][0:96, :], wqs[:, ch, h * 96:(h + 1) * 96], qnT[:, ch, :],
                                                       start=(ch == 0), stop=(ch == 2)),
                             reads=["wqs", "qnT%d" % ch], writes=["ps%d" % bank2])
                    P.dve(lambda e: e.tensor_tensor(out=qtmp[1][:], in0=ps[bank2][0:96, :], in1=ropeq[:, 1, :], op=ALU.mult),
                          reads=["ps%d" % bank2, "ropeq"], writes=["qtmp1"])
                    P.dve(lambda e: e.tensor_tensor(out=QT[s][:], in0=qtmp[0][:], in1=qtmp[1][:], op=ALU.add),
                          reads=["qtmp0", "qtmp1"], writes=["QT%d" % s])
                steps.append(qstep2)
            return steps

        LA = 2
        tiles_h = [(q0, nq, k0, kt, kt == nk // 128 - 1) for (q0, nq, k0, nk) in probs for kt in range(nk // 128)]
        n_t = len(tiles_h)
        for st_ in head_prep(0):
            st_()
        pvq = []
        prep_q = []
        timers = []
        ofirst = {}

        def emit_pv(item):
            (h, q0, nq, k0, kt, lastk, pslot, last_of_head) = item
            s = h % 2
            obank = 3 + s
            M = 65 if s == 0 else 128
            P.pe(lambda e: e.matmul(
                ps[obank][0:M, q0:q0 + nq], Vh[s][:, k0 // 128 + kt, 0:M], PT[pslot][:, 0:nq],
                start=(h not in ofirst), stop=lastk, skip_group_check=True),
                reads=["ar_c%d" % pslot, "Vh%d" % s], writes=["ps%d" % obank])
            ofirst[h] = True
            if last_of_head:
                p0 = 64 if s == 0 else 0
                ro = 0 if s == 0 else 64
                cc = h // 2
                P.dve(lambda e: e.reciprocal(out=rcp[s][p0:p0 + 1, :], in_=ps[obank][p0:p0 + 1, :]),
                      reads=["ps%d" % obank], writes=["rcp%d" % s])

                def norm2():
                    bcb = rotate("bc", [6])
                    if s == 0:
                        P.pe(lambda e: e.matmul(ps[bcb][0:64, :], onesel[64:65, 0:64], rcp[s][64:65, :], start=True, stop=True),
                             reads=["onesel", "rcp%d" % s], writes=["ps%d" % bcb])
                    else:
                        P.pe(lambda e: e.matmul(ps[bcb][:, :], onesel[0:1, :], rcp[s][0:1, :], start=True, stop=True),
                             reads=["onesel", "rcp%d" % s], writes=["ps%d" % bcb])
                    P.dve(lambda e: e.tensor_tensor(out=gb[ro:ro + 64, cc, :], in0=ps[obank][ro:ro + 64, :], in1=gb[ro:ro + 64, cc, :], op=ALU.mult),
                          reads=["ps%d" % obank, "gb%d" % cc], writes=["gb%d" % cc])
                    P.dve(lambda e: e.tensor_tensor(out=gb[ro:ro + 64, cc, :], in0=gb[ro:ro + 64, cc, :], in1=ps[bcb][ro:ro + 64, :], op=ALU.mult),
                          reads=["ps%d" % bcb, "gb%d" % cc], writes=["gb%d" % cc])
                timers.append([min(8, n_t), norm2])

        for h in range(8):
            s = h % 2
            ktk = ["ar_a", "ar_b"][s]
            while prep_q:
                prep_q.pop(0)()
            nxt_steps = head_prep(h + 1) if h < 7 else []
            per_iter = (len(nxt_steps) + max(1, n_t - LA) - 1) // max(1, n_t - LA) if nxt_steps else 0
            for it, (q0, nq, k0, kt, lastk) in enumerate(tiles_h):
                sbank = rotate("S", [0, 1, 2])
                P.pe(lambda e: e.matmul(
                    ps[sbank][:, 0:nq], KT[s][:, k0 + kt * 128:k0 + (kt + 1) * 128], QT[s][:, q0:q0 + nq], start=True, stop=True),
                    reads=[ktk, "QT%d" % s], writes=["ps%d" % sbank])
                pslot = rotate("PT", [0, 1, 2])
                P.act(lambda e: e.activation(
                    out=PT[pslot][:, 0:nq], in_=ps[sbank][:, 0:nq], func=AF.Exp, scale=SM_SCALE),
                    reads=["ps%d" % sbank], writes=["ar_c%d" % pslot])
                pvq.append((h, q0, nq, k0, kt, lastk, pslot, it == n_t - 1))
                if len(pvq) > LA:
                    emit_pv(pvq.pop(0))
                for tm in list(timers):
                    tm[0] -= 1
                    if tm[0] <= 0:
                        timers.remove(tm)
                        tm[1]()
                if it == LA - 1:
                    prep_q = nxt_steps
                if it >= LA:
                    for _ in range(per_iter):
                        if prep_q:
                            prep_q.pop(0)()
        while pvq:
            emit_pv(pvq.pop(0))
        for tm in timers:
            tm[1]()
        if l == 0:
            dump("yb%d" % g, gb[:], [128, 4, 512], ["gb%d" % c for c in range(4)], BF16)

        ck(9)
        if nxt is not None:
            load_layer_small(nxt[0], nxt[1])
        ysrc = [(ga, "ga"), (gb, "gb"), (gc, "gc")]
        for dc in range(8):
            b = wb_alloc()
            sbr = dc % 2
            P.dma("pool", lambda e, b=b, dc=dc: e.dma_start(out=WB[b][:, 0:3072], in_=pk(l, "gate%d" % dc)),
                  reads=(["agoutA%d" % l] if (samp and dc == 0) else []), writes=["WB%d" % b])
            P.dma("pool", lambda e, sbr=sbr, dc=dc: e.dma_start(out=wbr[sbr][:].rearrange("p a w -> p (a w)"), in_=pk(l, "br%d" % dc)),
                  writes=["wbr%d" % sbr])
            gv = WB[b][:, 0:3072].rearrange("p (k n w) -> p k n w", k=8, n=3)
            for n in range(3):
                for kc in range(8):
                    P.pe(lambda e, gv=gv, n=n, kc=kc: e.matmul(ps[n][:], gv[:, kc, n, :], hnT[:, kc, :],
                                                             start=(kc == 0), stop=(kc == 7)),
                         reads=["WB%d" % b] + HKall[kc], writes=["ps%d" % n])
                P.act(lambda e, n=n: e.activation(out=thg[n][:], in_=ps[n][:], func=AF.Tanh, scale=0.5),
                      reads=["ps%d" % n], writes=["thg%d" % n])
            wb_release(b)
            for n in range(3):
                yt, yk = ysrc[n]
                for cc in range(4):
                    P.pe(lambda e, n=n, cc=cc, sbr=sbr, yt=yt: e.matmul(ps[3 + n][:], wbr[sbr][:, n * 4 + cc, :], yt[:, cc, :],
                                                                        start=(cc == 0), stop=(cc == 3)),
                         reads=["wbr%d" % sbr, yk + "%d" % cc], writes=["ps%d" % (3 + n)])
            P.dve(lambda e: e.scalar_tensor_tensor(out=tn[0][:], in0=thg[0][:], scalar=1.0, in1=ps[3][:], op0=ALU.add, op1=ALU.mult),
                  reads=["thg0", "ps3"], writes=["tn0"])
            P.dve(lambda e: e.scalar_tensor_tensor(out=tn[1][:], in0=thg[1][:], scalar=1.0, in1=ps[4][:], op0=ALU.add, op1=ALU.mult),
                  reads=["thg1", "ps4"], writes=["tn1"])
            P.dve(lambda e: e.tensor_tensor(out=tn[0][:], in0=tn[0][:], in1=tn[1][:], op=ALU.add), reads=["tn0", "tn1"], writes=["tn0"])
            P.dve(lambda e: e.scalar_tensor_tensor(out=tn[1][:], in0=thg[2][:], scalar=1.0, in1=ps[5][:], op0=ALU.add, op1=ALU.mult),
                  reads=["thg2", "ps5"], writes=["tn1"])
            P.dve(lambda e, dc=dc: e.tensor_tensor(out=mrgT[:, dc, :], in0=tn[0][:], in1=tn[1][:], op=ALU.add),
                  reads=["tn0", "tn1"], writes=["mg%d" % dc])
        if l == 0:
            dump("mrg%d" % g, mrgT[:], [128, 8, 512], ["mg%d" % c for c in range(8)], BF16)

        ck(10)

    tail_state = {}

    def tail_setup(l, g):
        P.dma("sp", lambda e: e.dma_start(out=GG[:].rearrange("p (r j) -> p r j", r=4),
                                          in_=modsrc(l, g, 2).rearrange("r h p -> r (h p)").partition_broadcast(128)),
              reads=["agm_out%d" % l], writes=["GG"])
        bo = []
        for half in range(2):
            b = wb_alloc()
            P.dma("pool", lambda e, b=b, half=half: e.dma_start(out=WB[b][:, :], in_=pk(l, "wo%d" % half)),
                  writes=["WB%d" % b])
            bo.append(b)
        tail_state["bo"] = bo

    def tail_tile(l, g, tt, last):
        bo = tail_state["bo"]
        banks = [rotate("wo", [4, 5, 2, 3]) for _ in range(2)]
        for half in range(2):
            for kc in range(8):
                P.pe(lambda e, half=half, kc=kc, bank=banks[half]: e.matmul(
                    ps[bank][:], mrgT[:, kc, tt * 128:(tt + 1) * 128], WB8[bo[half]][:, kc, :], start=(kc == 0), stop=(kc == 7)),
                    reads=["mg%d" % kc, "WB%d" % bo[half]], writes=["ps%d" % banks[half]])
            P.act(lambda e, half=half, bank=banks[half]: e.activation(out=junk[:, 0:512], in_=ps[bank][:], func=AF.Square,
                                                                      accum_out=st_ss[:, half:half + 1]),
                  reads=["ps%d" % banks[half]], writes=["junk", "ss%d" % half])
        P.dve(lambda e: e.tensor_tensor(out=st_a[:, 0:1], in0=st_ss[:, 0:1], in1=st_ss[:, 1:2], op=ALU.add),
              reads=["ss0", "ss1"], writes=["st_a"])
        P.dve(lambda e: e.tensor_scalar(out=st_a[:, 0:1], in0=st_a[:, 0:1], scalar1=1.0 / 1024, scalar2=16.0 * EPS,
                                        op0=ALU.mult, op1=ALU.add), reads=["st_a"], writes=["st_a"])
        rsqrt(st_y[:, 0:1], st_a[:, 0:1], 1, ["st_a"], "st_y")
        for half in range(2):
            hs = slice(half * 512, (half + 1) * 512)
            P.dve(lambda e, half=half, hs=hs, bank=banks[half]: e.scalar_tensor_tensor(
                out=ttmp[half][:], in0=ps[bank][:], scalar=st_y[:, 0:1], in1=GG[:, hs], op0=ALU.mult, op1=ALU.mult),
                reads=["ps%d" % banks[half], "st_y", "GG"], writes=["ttmp%d" % half])
            P.dve(lambda e, half=half, hs=hs: e.tensor_tensor(out=X[:, tt, hs], in0=X[:, tt, hs], in1=ttmp[half][:], op=ALU.add),
                  reads=["ttmp%d" % half, "X%d" % tt], writes=["X%d" % tt])
        if last:
            outs.append(P.dma("sp", lambda e: e.dma_start(out=y_out[g][tt * 128:(tt + 1) * 128, :], in_=X[:, tt, :]),
                              reads=["X%d" % tt]))

    def tail_done(l, g):
        for b in tail_state["bo"]:
            wb_release(b)
        if l == 0:
            dump("x1_%d" % g, X[:], [128, 4, 1024], ["X%d" % t for t in range(4)])

    def cache_prep(l):
        P.dma("pool", lambda e: e.dma_start(out=cache_bf, in_=cache_d[l].rearrange("(t p) f -> p t f", p=128)),
              writes=["lat_bf"])
        bank = rotate("tr", [6, 7])
        bank2 = rotate("tr", [6, 7])
        for t in range(2):
            for ch in range(2):
                P.pe(lambda e, t=t, ch=ch: e.transpose(psb[bank][:, ch * 512 + t * 128:ch * 512 + (t + 1) * 128],
                                                       cache_bf[:, t, ch * 128:(ch + 1) * 128], ident[:]),
                     reads=["lat_bf", "ident"], writes=["ps%d" % bank])
            P.pe(lambda e, t=t: e.transpose(psb[bank2][0:96, t * 128:(t + 1) * 128], cache_bf[:, t, 192:288], ident[:]),
                 reads=["lat_bf", "ident"], writes=["ps%d" % bank2])
        P.act(lambda e: e.activation(out=ckvT[:, 0, 0:256], in_=psb[bank][:, 0:256], func=AF.Copy),
              reads=["ps%d" % bank], writes=["ck0_0"])
        P.dve(lambda e: e.tensor_copy(out=ckvT[:, 1, 0:256], in_=psb[bank][:, 512:768]), reads=["ps%d" % bank], writes=["ck0_1"])
        P.act(lambda e: e.activation(out=KR[64:96, 0:256], in_=psb[bank2][64:96, 0:256], func=AF.Copy),
              reads=["ps%d" % bank2], writes=["kr0"])

    try:
        emit_consts()
        ck(-2)
        passes = [(0, 0), (1, 0), (0, 1), (1, 1)]
        if stage is not None and stage >= 100:
            passes = [(0, 1), (1, 1)]

        def xload(g, tt):
            P.dma("sp", lambda e: e.dma_start(out=X[:, tt, :], in_=x_in[g][tt * 128:(tt + 1) * 128, :]), writes=["X%d" % tt])

        l0, g0 = passes[0]
        emit_mod(0)
        ck(-1)
        for tt in range(4):
            P.dma("pool", lambda e, tt=tt: e.dma_start(out=X[:, tt, :], in_=x_in[g0][tt * 128:(tt + 1) * 128, :]), writes=["X%d" % tt])
        for tt in range(4):
            prenorm_a(tt, 4 + tt)
        load_layer_small(l0, g0, "a")
        for tt in range(4):
            prenorm_b(tt, 4 + tt)
        load_layer_small(l0, g0, "b")
        b_kv_pre = None
        for i, (l, g) in enumerate(passes):
            if g == 1:
                cache_prep(l)
            nxt = passes[i + 1] if i + 1 < len(passes) else None
            if stage is not None and stage % 100 == 50:
                nxt = None
            group_layer(l, g, b_kv_pre, nxt)
            last = (l == 1)
            tail_setup(l, g)
            b_kv_pre = None
            if nxt is not None:
                b_kv_pre = load_win(nxt[0], OFF_KV, 288)
            for tt in range(4):
                tail_tile(l, g, tt, last)
                if nxt is not None:
                    if nxt[1] != g:
                        xload(nxt[1], tt)
                    if tt >= 1:
                        prenorm_tile(nxt[0], nxt[1], tt - 1)
            if nxt is not None:
                prenorm_tile(nxt[0], nxt[1], 3)
            tail_done(l, g)
            if stage is not None and stage % 100 == 50:
                raise _Stop()
    except _Stop:
        pass
    stats = P.emit(final_wait_ops=[o.idx for o in outs])
    return nc, stats, dbg


def _pool_matrix(S):
    mats = []
    t = np.arange(S)
    for win in POOL_WINDOWS:
        lo = np.clip(t - win // 2, 0, S)
        hi = np.clip(t + win - win // 2, 0, S)
        A = np.zeros((S, S), np.float32)
        for tt in range(S):
            A[lo[tt]:hi[tt], tt] = 1.0 / float(hi[tt] - lo[tt])
        A[t, t] -= 1.0
        mats.append(A)
    return mats


def _core_consts(qd):
    c = {}
    c["c_ident"] = np.eye(128, dtype=np.float32)
    pos = qd * 512 + np.arange(512)
    row = (pos // 64).astype(np.float32)
    col = (pos % 64).astype(np.float32)
    inv = (1.0 / (10000.0 ** (np.arange(0, 16, 2, dtype=np.float32) / 16.0))).astype(np.float32)
    ang = np.stack([row[:, None] * inv, col[:, None] * inv], axis=1).astype(np.float32)
    cs, sn = np.cos(ang).astype(np.float32), np.sin(ang).astype(np.float32)
    cos2 = np.stack([cs, cs], axis=2).reshape(512, 32)
    sin2 = np.stack([-sn, sn], axis=2).reshape(512, 32)
    rk = np.stack([cos2, sin2], axis=0).reshape(2, 4, 128, 32).transpose(2, 0, 1, 3)
    c["c_ropek"] = np.ascontiguousarray(rk)
    rq = np.zeros((96, 2, 512), np.float32)
    rq[0:64, 0, :] = 1.0
    rq[64:96, 0, :] = cos2.T
    rq[64:96, 1, :] = sin2.T
    c["c_ropeq"] = rq
    Ap = _pool_matrix(256)
    pp = np.zeros((128, 16, 128), np.float32)
    for gi in range(4):
        for k, (i, j) in enumerate([(0, 0), (1, 0), (0, 1), (1, 1)]):
            pp[:, gi * 4 + k, :] = Ap[gi][i * 128:(i + 1) * 128, j * 128:(j + 1) * 128]
    c["c_poolp"] = pp
    As = _pool_matrix(2048)
    pairs = [(i, j) for j in range(4) for i in range(4) if abs(i - j) <= 1]
    psm = np.zeros((128, 40, 128), np.float32)
    base = qd * 512
    for gi in range(4):
        for k, (i, j) in enumerate(pairs):
            psm[:, gi * 10 + k, :] = As[gi][base + i * 128:base + (i + 1) * 128, base + j * 128:base + (j + 1) * 128]
    c["c_pools"] = psm
    hl = np.zeros((72, 8, 128), np.float32)
    sel = np.zeros((72, 2), np.float32)
    for gi in range(4):
        if qd > 0:
            for m in range(8):
                hl[(qd - 1) * 18 + 8 + m, gi * 2 + 0, :] = As[gi][base - 8 + m, base:base + 128]
        if qd < 3:
            for m in range(8):
                hl[(qd + 1) * 18 + m, gi * 2 + 1, :] = As[gi][base + 512 + m, base + 384:base + 512]
    if qd > 0:
        sel[(qd - 1) * 18 + 17, 0] = 1.0
    if qd < 3:
        sel[(qd + 1) * 18 + 16, 1] = 1.0
    c["c_halo"] = hl
    c["c_selc"] = sel
    return c


def _mod_vec(b_mod, g_pre, g_post, r):
    v = np.empty((2, 1280), np.float32)
    q = slice(r * 256, (r + 1) * 256)
    for l in range(2):
        b3 = b_mod[l].reshape(3, 1024)
        v[l, 0:256] = b3[0, q]
        v[l, 256:512] = b3[1, q]
        v[l, 512:768] = b3[2, q]
        v[l, 768:1024] = g_pre[l, q]
        v[l, 1024:1280] = g_post[l, q]
    return v


_CACHE = {}


def _get_prog(debug=False, stage=None):
    if (debug, stage) not in _CACHE:
        _CACHE[(debug, stage)] = build(debug, stage)
    return _CACHE[(debug, stage)]


def kernel(x_prompt, x_sample, cache_mla_latent, c, c_ctx, w_mod, b_mod, g_pre, g_post, w_in, conv_w, conv_b,
           g_q, w_uq, g_kv, w_ukv, pool_w, pool_scale, w_branch, w_o, _debug=False, _stage=None):
    f = lambda a: np.ascontiguousarray(np.asarray(a, dtype=np.float32))
    x_prompt, x_sample, cache_mla_latent, c, c_ctx = map(f, (x_prompt, x_sample, cache_mla_latent, c, c_ctx))
    shared = dict(b_mod=f(b_mod), g_pre=f(g_pre), g_post=f(g_post), conv_w=f(conv_w), conv_b=f(conv_b), g_q=f(g_q),
                  g_kv=f(g_kv), pool_scale=f(pool_scale),
                  wpk=_pack_weights(f(w_mod), f(w_in), f(w_uq), f(w_ukv), f(pool_w), f(w_branch), f(w_o)))
    nc, stats, dbg = _get_prog(_debug, _stage)
    in_maps = []
    wpk_rank = []
    for r in range(4):
        w = shared["wpk"].copy()
        _pack_mod(w, f(w_mod), r)
        wpk_rank.append(w)
    for i in range(8):
        b, qd = i // 4, i % 4
        m = dict(shared)
        m["wpk"] = wpk_rank[qd]
        m["modvec"] = _mod_vec(shared["b_mod"], shared["g_pre"], shared["g_post"], qd)
        m["xp"] = np.ascontiguousarray(x_prompt[2 * i:2 * i + 2].reshape(512, 1024))
        m["xs"] = np.ascontiguousarray(x_sample[b, qd * 512:(qd + 1) * 512])
        m["cache"] = np.ascontiguousarray(cache_mla_latent[b])
        m["cond"] = np.ascontiguousarray(np.stack([c_ctx, c[b]], axis=0))
        m.update(_core_consts(qd))
        in_maps.append(m)
    res = run_bass_kernel_spmd(nc, in_maps, core_ids=list(range(8)))
    R = res.results
    y_prompt = np.stack([R[i]["yp"].reshape(2, 256, 1024) for i in range(8)], 0).reshape(16, 256, 1024)
    y_sample = np.stack([R[i]["ys"] for i in range(8)], 0).reshape(2, 2048, 1024)
    st = np.stack([R[i]["lat"].reshape(2, 2, 256, 288).transpose(1, 0, 2, 3) for i in range(8)], 0).reshape(16, 2, 256, 288)
    outs = (y_prompt.astype(np.float32), y_sample.astype(np.float32), st.astype(np.float32))
    if _debug:
        return outs, R
    return outs
```

```python
import numpy as np
import ml_dtypes
import concourse.bass as bass
import concourse.mybir as mybir
from concourse.bass_utils import run_bass_kernel_spmd

F32 = mybir.dt.float32
BF16 = mybir.dt.bfloat16
I32 = mybir.dt.int32
AF = mybir.ActivationFunctionType
ALU = mybir.AluOpType

D_MODEL = 1024
DEPTH = 2
EPS = 1e-6
W_IN_COLS = 7328
OFF_AB, OFF_AC, OFF_AX, OFF_AZ, OFF_Q, OFF_KV, OFF_BZ, OFF_CU, OFF_CZ, OFF_MG = (
    0, 512, 1024, 1536, 2048, 2432, 2720, 3232, 3744, 4256)
NK_S = 2304
SM_SCALE = 96 ** -0.5
POOL_WINDOWS = (2, 4, 8, 16)

ENGS = ("pe", "act", "dve", "pool", "sp")
DMA_RING = 8

W_BLOCKS = [("kv", OFF_KV, 288), ("cu", OFF_CU, 512), ("ac", OFF_AC, 512), ("ax", OFF_AX, 512), ("q", OFF_Q, 384),
            ("az", OFF_AZ, 512), ("ab", OFF_AB, 512), ("bz", OFF_BZ, 512), ("cz", OFF_CZ, 512)]


def _pack_layout():
    off = {}
    cur = 0

    def add(name, n):
        nonlocal cur
        off[name] = (cur, n)
        cur += n
    for nm, c0, W in W_BLOCKS:
        add("win_" + nm, 8 * W)
    for dc in range(8):
        add("gate%d" % dc, 8 * 3 * 128)
        add("br%d" % dc, 3 * 4 * 128)
    for half in range(2):
        add("wo%d" % half, 8 * 512)
    add("modA", 8 * 512)
    add("modB", 8 * 256)
    add("wuq", 3 * 768)
    add("wukv", 2 * 1024)
    add("poolw", 4 * 128)
    return off, cur


PK_OFF, PK_COLS = _pack_layout()


def _pack_mod(out, w_mod, r):
    for l in range(2):
        wm = w_mod[l].reshape(8, 128, 3, 1024)[:, :, :, r * 256:(r + 1) * 256]
        o, n = PK_OFF["modA"]
        out[l, :, o:o + n] = wm[:, :, 0:2, :].transpose(1, 0, 2, 3).reshape(128, n)
        o, n = PK_OFF["modB"]
        out[l, :, o:o + n] = wm[:, :, 2, :].transpose(1, 0, 2).reshape(128, n)


def _pack_weights(w_mod, w_in, w_uq, w_ukv, pool_w, w_branch, w_o):
    out = np.empty((2, 128, PK_COLS), np.float32)
    for l in range(2):
        def put(name, arr):
            o, n = PK_OFF[name]
            out[l, :, o:o + n] = arr.reshape(128, n)
        win = w_in[l].reshape(8, 128, W_IN_COLS)
        for nm, c0, W in W_BLOCKS:
            put("win_" + nm, win[:, :, c0:c0 + W].transpose(1, 0, 2))
        mg = win[:, :, OFF_MG:OFF_MG + 3072].reshape(8, 128, 3, 8, 128)
        wb = w_branch[l].reshape(3, 4, 128, 8, 128)
        for dc in range(8):
            put("gate%d" % dc, mg[:, :, :, dc, :].transpose(1, 0, 2, 3))
            put("br%d" % dc, wb[:, :, :, dc, :].transpose(2, 0, 1, 3))
        wo = w_o[l].reshape(8, 128, 1024)
        for half in range(2):
            put("wo%d" % half, wo[:, :, half * 512:(half + 1) * 512].transpose(1, 0, 2))
        put("wuq", w_uq[l].reshape(3, 128, 768).transpose(1, 0, 2))
        put("wukv", w_ukv[l].reshape(2, 128, 1024).transpose(1, 0, 2))
        put("poolw", pool_w[l].transpose(1, 0, 2))
    return out


class _Rec:
    def __init__(self):
        self.call = None

    def __getattr__(self, name):
        def f(*a, **kw):
            assert self.call is None
            self.call = (name, a, kw)
            return self
        return f


class _Op:
    __slots__ = ("eng", "fn", "reads", "writes", "kind", "deps", "idx", "sig", "sem", "val", "clock")

    def __init__(self, eng, fn, reads, writes, kind):
        rec = _Rec()
        fn(rec)
        name, a, kw = rec.call
        self.eng, self.reads, self.writes, self.kind = eng, reads, writes, kind
        self.fn = lambda e: getattr(e, name)(*a, **kw)
        self.deps, self.sig, self.sem, self.val, self.clock = [], False, None, 0, None


class Prog:
    def __init__(self, nc):
        self.nc = nc
        self.ops = []
        self.last_w = {}
        self.readers = {}
        self.engobj = {"pe": nc.tensor, "act": nc.scalar, "dve": nc.vector, "pool": nc.gpsimd, "sp": nc.sync}

    def op(self, eng, fn, reads=(), writes=(), kind="c"):
        o = _Op(eng, fn, tuple(reads), tuple(writes), kind)
        o.idx = len(self.ops)
        deps = set()
        for r in o.reads:
            w = self.last_w.get(r)
            if w is not None:
                deps.add(w)
            if r.startswith("ps"):
                for rd in self.readers.get(r, ()):
                    if self.ops[rd].eng != eng:
                        deps.add(rd)
        for w_ in o.writes:
            w = self.last_w.get(w_)
            if w is not None:
                deps.add(w)
            deps.update(self.readers.get(w_, ()))
        deps.discard(o.idx)
        latest = {}
        keep = []
        for d in deps:
            p = self.ops[d]
            if p.kind == "c":
                if p.eng not in latest or latest[p.eng] < d:
                    latest[p.eng] = d
            else:
                keep.append(d)
        o.deps = sorted(keep + list(latest.values()))
        for r in o.reads:
            self.readers.setdefault(r, []).append(o.idx)
        for w_ in o.writes:
            self.last_w[w_] = o.idx
            self.readers[w_] = []
        self.ops.append(o)
        return o

    def pe(self, fn, reads=(), writes=()):
        return self.op("pe", fn, reads, writes)

    def act(self, fn, reads=(), writes=()):
        return self.op("act", fn, reads, writes)

    def dve(self, fn, reads=(), writes=()):
        return self.op("dve", fn, reads, writes)

    def dma(self, eng, fn, reads=(), writes=()):
        return self.op(eng, fn, reads, writes, "d")

    def cc(self, fn, reads=(), writes=()):
        return self.op("pool", fn, reads, writes, "cc")

    @staticmethod
    def _skip(p, o):
        return p.eng == "pe" and o.eng == "pe" and p.kind == "c" and o.kind == "c"

    def emit(self, final_wait_ops=()):
        nc, ops = self.nc, self.ops
        for o in ops:
            for d in o.deps:
                if not self._skip(ops[d], o):
                    ops[d].sig = True
        for i in final_wait_ops:
            ops[i].sig = True
        esem = {e: nc.alloc_semaphore("s_" + e) for e in ENGS}
        ccsem = nc.alloc_semaphore("s_cc")
        rings = {e: [nc.alloc_semaphore("r_%s%d" % (e, i)) for i in range(DMA_RING)] for e in ("sp", "pool", "act")}
        ecount = {e: 0 for e in ENGS}
        cccount = 0
        dcount = {e: 0 for e in rings}
        known = {e: {} for e in ENGS}
        nwaits = 0

        def wait(eng, sem, val, clock):
            nonlocal nwaits
            k = known[eng]
            if k.get(sem.name, 0) >= val:
                return
            self.engobj[eng].wait_ge(sem, val)
            nwaits += 1
            if clock is not None:
                for s, v in clock.items():
                    if k.get(s, 0) < v:
                        k[s] = v
            if k.get(sem.name, 0) < val:
                k[sem.name] = val

        for o in ops:
            e = o.eng
            if o.kind == "d":
                i = dcount[e]
                s = rings[e][i % DMA_RING]
                prev = (i // DMA_RING) * 16
                if prev > 0:
                    wait(e, s, prev, None)
            for d in o.deps:
                p = ops[d]
                if self._skip(p, o):
                    continue
                wait(e, p.sem, p.val, p.clock)
            ins = o.fn(self.engobj[e])
            if o.kind == "d":
                i = dcount[e]
                s = rings[e][i % DMA_RING]
                v = (i // DMA_RING + 1) * 16
                ins.then_inc(s, 16)
                dcount[e] += 1
                o.sem, o.val = s, v
                ck = dict(known[e])
                ck[s.name] = v
                o.clock = ck
            elif o.kind == "cc":
                cccount += 1
                ins.then_inc(ccsem, 1)
                o.sem, o.val = ccsem, cccount
                ck = dict(known[e])
                ck[ccsem.name] = cccount
                o.clock = ck
            elif o.sig:
                ecount[e] += 1
                ins.then_inc(esem[e], 1)
                o.sem, o.val = esem[e], ecount[e]
                ck = dict(known[e])
                ck[esem[e].name] = ecount[e]
                o.clock = ck
        for i in final_wait_ops:
            p = ops[i]
            wait("sp", p.sem, p.val, p.clock)
        for e in rings:
            for k, s in enumerate(rings[e]):
                n = (dcount[e] - k + DMA_RING - 1) // DMA_RING if dcount[e] > k else 0
                if n > 0:
                    wait("sp", s, 16 * n, None)
        if cccount:
            wait("sp", ccsem, cccount, None)
        return dict(nops=len(ops), nwaits=nwaits, ecount=ecount, dcount=dcount)


class _Stop(Exception):
    pass


def build(debug=False, stage=None):
    nc = bass.Bass("TRN2", target_bir_lowering=False)
    P = Prog(nc)

    def ck(n):
        if stage is not None and n >= stage:
            raise _Stop()

    def din(name, shape, dt=F32):
        return nc.dram_tensor(name, list(shape), dt, kind="ExternalInput").ap()

    def dout(name, shape, dt=F32):
        return nc.dram_tensor(name, list(shape), dt, kind="ExternalOutput").ap()

    x_in = [din("xp", [512, 1024]), din("xs", [512, 1024])]
    cache_d = din("cache", [2, 256, 288])
    cond_d = din("cond", [2, 1024])
    wpk = din("wpk", [2, 128, PK_COLS])
    b_mod = din("b_mod", [2, 3072])
    g_pre = din("g_pre", [2, 1024])
    g_post = din("g_post", [2, 1024])
    conv_w = din("conv_w", [2, 3, 512])
    conv_b = din("conv_b", [2, 512])
    g_q = din("g_q", [2, 384])
    g_kv = din("g_kv", [2, 256])
    pool_scale = din("pool_scale", [2, 512])
    c_ident = din("c_ident", [128, 128])
    c_ropek = din("c_ropek", [128, 2, 4, 32])
    c_ropeq = din("c_ropeq", [96, 2, 512])
    c_poolp = din("c_poolp", [128, 16, 128])
    c_pools = din("c_pools", [128, 40, 128])
    c_halo = din("c_halo", [72, 8, 128])
    c_selc = din("c_selc", [72, 2])
    y_out = [dout("yp", [512, 1024]), dout("ys", [512, 1024])]
    lat_o = dout("lat", [2, 512, 288])
    modvec = din("modvec", [2, 1280])
    agm_in = [nc.dram_tensor("agm_in%d" % l, [2, 768], F32).ap() for l in range(2)]
    agm_out = [nc.dram_tensor("agm_out%d" % l, [8, 768], F32).ap() for l in range(2)]
    aginA = [nc.dram_tensor("aginA%d" % l, [288, 512], BF16).ap() for l in range(2)]
    agoutA = [nc.dram_tensor("agoutA%d" % l, [4 * 288, 512], BF16).ap() for l in range(2)]
    aginB = [nc.dram_tensor("aginB%d" % l, [18, 512], BF16).ap() for l in range(2)]
    agoutB = [nc.dram_tensor("agoutB%d" % l, [4 * 18, 512], BF16).ap() for l in range(2)]

    dbg = {}

    def sb(name, shape, dt=F32):
        return nc.alloc_sbuf_tensor("sb_" + name, list(shape), dt)

    X = sb("X", [128, 4, 1024])
    xn = [sb("xn%d" % i, [128, 1024], BF16) for i in range(2)]
    junk = sb("junk", [128, 1024], BF16)
    hnT = sb("hnT", [128, 8, 512], BF16)
    mrgT = sb("mrgT", [128, 8, 512], BF16)
    NWB = 4
    WB = [sb("WB%d" % i, [128, 4096], BF16) for i in range(NWB)]
    WB8 = [w[:, :].rearrange("p (k w) -> p k w", k=8) for w in WB]
    wbr = [sb("wbr%d" % i, [128, 12, 128], BF16) for i in range(2)]
    wq = sb("wq", [128, 3, 768], BF16)
    wqs = sb("wqs", [128, 3, 768], BF16)
    wkv = sb("wkv", [128, 2, 1024], BF16)
    wpool = sb("wpool", [128, 4, 128], BF16)
    thg = [sb("thg%d" % i, [128, 512], BF16) for i in range(3)]
    ga = sb("ga", [128, 4, 512], BF16)
    gb = sb("gb", [128, 4, 512], BF16)
    gc = sb("gc", [128, 4, 512], BF16)
    tht = [sb("tht%d" % i, [128, 512], BF16) for i in range(2)]
    ARW = 2304
    arena = sb("arena", [128, 3 * ARW], BF16)
    acT = arena[:, 0:2048].rearrange("p (c t) -> p c t", c=4)
    cu_tm = arena[:, ARW:ARW + 2048].rearrange("p (c t) -> p c t", c=4)
    uT = arena[:, 2 * ARW:2 * ARW + 2064].rearrange("p (c t) -> p c t", c=4)
    KT = [arena[0:96, 0:ARW], arena[0:96, ARW:2 * ARW]]
    PT = [arena[:, 2 * ARW + i * 512:2 * ARW + (i + 1) * 512] for i in range(3)]
    pooledT = [sb("pooledT%d" % i, [128, 512], BF16) for i in range(2)]
    lat = sb("lat", [128, 4, 288])
    lat_bf = sb("lat_bf", [128, 4, 288], BF16)
    qn = sb("qn", [128, 4, 384], BF16)
    qnT = sb("qnT", [128, 3, 512], BF16)
    ckvT = sb("ckvT", [128, 2, NK_S], BF16)
    KR = sb("KR", [128, NK_S], BF16)
    H = sb("H", [72, 512], BF16)
    Vh = [sb("Vh0", [128, 18, 65], BF16), sb("Vh1", [128, 18, 128], BF16)]
    rcp1 = sb("rcp", [128, 512])
    rcp = [rcp1, rcp1]
    onesel = sb("onesel", [128, 128])
    QT = [sb("QT%d" % i, [96, 512], BF16) for i in range(2)]
    qtmp = [sb("qtmp%d" % i, [96, 512]) for i in range(2)]
    tn = [sb("tn%d" % i, [128, 512]) for i in range(2)]
    ttmp = [sb("ttmp%d" % i, [128, 512]) for i in range(2)]
    mrow = [qtmp[i][0:2, :] for i in range(2)]
    mbc = [tn[i][0:2, :] for i in range(2)]
    mbc2 = [ttmp[i][0:2, :] for i in range(2)]
    GG = sb("GG", [128, 1024])
    G1 = sb("G1", [128, 8])
    SH = sb("SH", [128, 8])
    gq_bc = sb("gq_bc", [128, 384])
    gkv_bc = sb("gkv_bc", [128, 256])
    cwT = sb("cwT", [128, 3, 4])
    cbT = sb("cbT", [128, 4])
    psT = sb("psT", [128, 4])
    Dg = sb("Dg", [128, 12, 128], BF16)
    ident = sb("ident", [128, 128], BF16)
    ropek = sb("ropek", [128, 2, 4, 32])
    ropeq = sb("ropeq", [96, 2, 512])
    rk1 = sb("rk1", [128, 4, 32])
    rk2 = sb("rk2", [128, 4, 32])
    poolp = sb("poolp", [128, 16, 128], BF16)
    pools = sb("pools", [128, 40, 128], BF16)
    halo = sb("halo", [72, 8, 128], BF16)
    selc = sb("selc", [72, 2], BF16)
    cache_bf = lat_bf[:, 0:2, :]
    condT = sb("condT", [128, 8, 2])
    condth = sb("condth", [128, 8, 2])
    condS = sb("condS", [128, 8, 2], BF16)
    uh = sb("uh", [2, 512], BF16)
    uha = ttmp[1][0:2, :]
    st_ss = sb("st_ss", [128, 4])
    st_a = sb("st_a", [128, 4])
    st_y = sb("st_y", [128, 4])
    negh = sb("negh", [128, 4])
    ps = [nc.alloc_psum_tensor("ps%d" % i, [128, 512], F32) for i in range(8)]
    psb = [p_[:].bitcast(BF16) for p_ in ps]

    outs = []

    def dump(name, ap, shape, keys, dt=F32):
        if not debug:
            return
        d = dout("dbg_" + name, shape, dt)
        dbg[name] = d
        outs.append(P.dma("sp", lambda e: e.dma_start(out=d, in_=ap), reads=keys))

    wb_next = [0]
    wb_busy = [False] * NWB

    def wb_alloc():
        for _ in range(NWB):
            b = wb_next[0] % NWB
            wb_next[0] += 1
            if not wb_busy[b]:
                wb_busy[b] = True
                return b
        raise RuntimeError("no free weight buffer")

    def wb_release(b):
        wb_busy[b] = False

    rot = {}

    def rotate(name, banks):
        i = rot.get(name, 0)
        rot[name] = i + 1
        return banks[i % len(banks)]

    def pk(l, name):
        o, n = PK_OFF[name]
        return wpk[l, :, o:o + n]

    def rsqrt(y_ap, a_ap, k, rkeys, wkey, iters=3):
        P.op("pool", lambda e: e.tensor_tensor(out=y_ap, in0=a_ap, in1=negh[:, 0:k], op=ALU.pow),
             reads=list(rkeys) + ["negh"], writes=[wkey])

    HK = ["hn%d" % k for k in range(8)]
    ARK = ["ar_a", "ar_b", "ar_c0", "ar_c1", "ar_c2"]

    def emit_consts():
        P.dma("pool", lambda e: e.dma_start(out=ident[:], in_=c_ident), writes=["ident"])
        P.dma("sp", lambda e: e.dma_start(out=ropek[:], in_=c_ropek), writes=["ropek"])
        P.dma("sp", lambda e: e.dma_start(out=ropeq[:], in_=c_ropeq), writes=["ropeq"])
        P.dve(lambda e: e.memset(negh[:], -0.5), writes=["negh"])
        P.dve(lambda e: e.memset(Vh[0][:], 1.0), writes=["Vh0"])
        P.dve(lambda e: e.memset(Vh[1][:], 0.0), writes=["Vh1"])
        P.dve(lambda e: e.memset(Vh[1][:, :, 0:1], 1.0), writes=["Vh1"])
        P.dve(lambda e: e.memset(onesel[:], 0.0), writes=["onesel"])
        P.dve(lambda e: e.memset(onesel[64:65, 0:64], 1.0), writes=["onesel"])
        P.dve(lambda e: e.memset(onesel[0:1, 64:128], 1.0), writes=["onesel"])
        P.dve(lambda e: e.memset(rcp1[:], 1.0), writes=["rcp0", "rcp1"])
        for ci in range(2):
            P.dma("sp", lambda e, ci=ci: e.dma_start(out=condT[:, :, ci], in_=cond_d[ci].rearrange("(k p) -> p k", p=128),
                                                     allow_slow_non_contiguous=True), writes=["condT"])
        P.act(lambda e: e.activation(out=condth[:], in_=condT[:], func=AF.Tanh, scale=0.5),
              reads=["condT"], writes=["condth"])
        P.dve(lambda e: e.scalar_tensor_tensor(out=condth[:], in0=condth[:], scalar=1.0, in1=condT[:],
                                               op0=ALU.add, op1=ALU.mult), reads=["condth", "condT"], writes=["condth"])
        P.dve(lambda e: e.tensor_scalar(out=condS[:], in0=condth[:], scalar1=0.5, scalar2=None, op0=ALU.mult),
              reads=["condth"], writes=["condS"])

    def emit_mod(l, blks=None):
        bA = wb_alloc()
        P.dma("pool", lambda e: e.dma_start(out=WB[bA][:, :], in_=pk(l, "modA")), writes=["WB%d" % bA])
        bB = wb_alloc()
        P.dma("pool", lambda e: e.dma_start(out=WB[bB][:, 0:2048], in_=pk(l, "modB")), writes=["WB%d" % bB])
        vB = WB[bB][:, 0:2048].rearrange("p (k w) -> p k w", k=8)
        P.dma("sp", lambda e: e.dma_start(out=mbc[0], in_=modvec[l, 0:512].partition_broadcast(2)), writes=["tn0"])
        P.dma("sp", lambda e: e.dma_start(out=mbc[1][:, 0:256], in_=modvec[l, 512:768].partition_broadcast(2)), writes=["tn1"])
        P.dma("sp", lambda e: e.dma_start(out=mbc2[0][:, 0:256], in_=modvec[l, 768:1024].partition_broadcast(2)), writes=["ttmp0"])
        P.dma("sp", lambda e: e.dma_start(out=mbc2[1][:, 0:256], in_=modvec[l, 1024:1280].partition_broadcast(2)), writes=["ttmp1"])
        for kc in range(8):
            P.pe(lambda e, kc=kc: e.matmul(ps[0][0:2, :], condS[:, kc, :], WB8[bA][:, kc, :], start=(kc == 0), stop=(kc == 7)),
                 reads=["condS", "WB%d" % bA], writes=["ps0"])
        for kc in range(8):
            P.pe(lambda e, kc=kc: e.matmul(ps[1][0:2, 0:256], condS[:, kc, :], vB[:, kc, :], start=(kc == 0), stop=(kc == 7)),
                 reads=["condS", "WB%d" % bB], writes=["ps1"])
        wb_release(bA)
        wb_release(bB)
        P.dve(lambda e: e.tensor_tensor(out=mrow[0], in0=ps[0][0:2, :], in1=mbc[0], op=ALU.add),
              reads=["ps0", "tn0"], writes=["qtmp0"])
        P.dve(lambda e: e.scalar_tensor_tensor(out=mrow[0][:, 256:512], in0=mrow[0][:, 256:512], scalar=1.0, in1=mbc2[0][:, 0:256],
                                               op0=ALU.add, op1=ALU.mult), reads=["qtmp0", "ttmp0"], writes=["qtmp0"])
        P.dve(lambda e: e.tensor_tensor(out=mrow[1][:, 0:256], in0=ps[1][0:2, 0:256], in1=mbc[1][:, 0:256], op=ALU.add),
              reads=["ps1", "tn1"], writes=["qtmp1"])
        P.dve(lambda e: e.tensor_tensor(out=mrow[1][:, 0:256], in0=mrow[1][:, 0:256], in1=mbc2[1][:, 0:256], op=ALU.mult),
              reads=["qtmp1", "ttmp1"], writes=["qtmp1"])
        P.dma("sp", lambda e: e.dma_start(out=agm_in[l][:, 0:512], in_=mrow[0]), reads=["qtmp0"], writes=["agm_in%da" % l])
        P.dma("sp", lambda e: e.dma_start(out=agm_in[l][:, 512:768], in_=mrow[1][:, 0:256]), reads=["qtmp1"], writes=["agm_in%db" % l])
        P.cc(lambda e: e.collective_compute("AllGather", ALU.bypass, replica_groups=[[0, 1, 2, 3], [4, 5, 6, 7]],
                                            ins=[agm_in[l]], outs=[agm_out[l]]),
             reads=["agm_in%da" % l, "agm_in%db" % l], writes=["agm_out%d" % l])

    def modsrc(l, cond, t):
        return agm_out[l].rearrange("(r c) (t h p) -> r c t h p", c=2, t=3, h=2)[:, cond, t, :, :]

    def load_layer_small(l, cond):
        for h in range(2):
            P.dma("sp", lambda e, h=h: e.dma_start(out=SH[:, h:8:2], in_=modsrc(l, cond, 0)[:, h, :].rearrange("r p -> p r"),
                                                   allow_slow_non_contiguous=True), reads=["agm_out%d" % l], writes=["SH"])
            P.dma("sp", lambda e, h=h: e.dma_start(out=G1[:, h:8:2], in_=modsrc(l, cond, 1)[:, h, :].rearrange("r p -> p r"),
                                                   allow_slow_non_contiguous=True), reads=["agm_out%d" % l], writes=["G1"])
        P.dma("sp", lambda e: e.dma_start(out=gq_bc[:], in_=g_q[l].partition_broadcast(128)), writes=["gq_bc"])
        P.dma("sp", lambda e: e.dma_start(out=gkv_bc[:], in_=g_kv[l].partition_broadcast(128)), writes=["gkv_bc"])
        for k in range(3):
            P.dma("sp", lambda e, k=k: e.dma_start(out=cwT[:, k, :], in_=conv_w[l, k].rearrange("(c p) -> p c", p=128),
                                                   allow_slow_non_contiguous=True), writes=["cwT"])
        P.dma("sp", lambda e: e.dma_start(out=cbT[:], in_=conv_b[l].rearrange("(c p) -> p c", p=128),
                                          allow_slow_non_contiguous=True), writes=["cbT"])
        P.dma("sp", lambda e: e.dma_start(out=psT[:], in_=pool_scale[l].rearrange("(c p) -> p c", p=128),
                                          allow_slow_non_contiguous=True), writes=["psT"])
        for cc in range(4):
            for k in range(3):
                P.dve(lambda e, cc=cc, k=k: e.tensor_scalar(out=Dg[:, cc * 3 + k, :], in0=ident[:],
                                                            scalar1=cwT[:, k, cc:cc + 1], scalar2=None, op0=ALU.mult),
                      reads=["ident", "cwT"], writes=["Dg"])

    pn_ss = sb("pn_ss", [128, 4])
    pn_a = sb("pn_a", [128, 4])
    pn_y = sb("pn_y", [128, 4])

    def prenorm_a(tt, bank):
        c = slice(tt, tt + 1)
        P.act(lambda e: e.activation(out=junk[:], in_=X[:, tt, :], func=AF.Square, accum_out=pn_ss[:, c]),
              reads=["X%d" % tt], writes=["junk", "pss%d" % tt])
        P.dve(lambda e: e.tensor_scalar(out=pn_a[:, c], in0=pn_ss[:, c], scalar1=1.0 / 1024, scalar2=EPS,
                                        op0=ALU.mult, op1=ALU.add), reads=["pss%d" % tt], writes=["pa%d" % tt])
        rsqrt(pn_y[:, c], pn_a[:, c], 1, ["pa%d" % tt], "py%d" % tt)
        s = tt % 2
        P.act(lambda e: e.activation(out=xn[s][:], in_=X[:, tt, :], func=AF.Identity, scale=pn_y[:, c]),
              reads=["X%d" % tt, "py%d" % tt], writes=["xn%d" % s])
        for kc in range(8):
            P.pe(lambda e, kc=kc: e.transpose(psb[bank][:, kc * 128:(kc + 1) * 128], xn[s][:, kc * 128:(kc + 1) * 128], ident[:]),
                 reads=["xn%d" % s, "ident"], writes=["ps%d" % bank])

    def prenorm_b(tt, bank):
        for kc in range(8):
            dst = hnT[:, kc, tt * 128:(tt + 1) * 128]
            src = psb[bank][:, kc * 128:(kc + 1) * 128]
            if tt % 2 == 0:
                P.act(lambda e, dst=dst, src=src, kc=kc: e.activation(out=dst, in_=src, func=AF.Identity,
                                                                      scale=G1[:, kc:kc + 1], bias=SH[:, kc:kc + 1]),
                      reads=["ps%d" % bank, "G1", "SH"], writes=[HK[kc] + "_%d" % tt])
            else:
                P.dve(lambda e, dst=dst, src=src, kc=kc: e.tensor_scalar(out=dst, in0=src, scalar1=G1[:, kc:kc + 1],
                                                                         scalar2=SH[:, kc:kc + 1], op0=ALU.mult, op1=ALU.add),
                      reads=["ps%d" % bank, "G1", "SH"], writes=[HK[kc] + "_%d" % tt])

    def prenorm_tile(l, g, tt):
        bank = rotate("tr", [6, 7])
        prenorm_a(tt, bank)
        prenorm_b(tt, bank)

    wname = {c0: nm for nm, c0, W in W_BLOCKS}
    wview = {}

    def load_win(l, c0, W):
        b = wb_alloc()
        P.dma("pool", lambda e: e.dma_start(out=WB[b][:, 0:8 * W], in_=pk(l, "win_" + wname[c0])), writes=["WB%d" % b])
        wview[b] = WB[b][:, 0:8 * W].rearrange("p (k w) -> p k w", k=8)
        return b

    consts_loaded = []

    def group_layer(l, g, b_kv_pre=None, nxt=None):
        samp = g == 1
        HKall = [[HK[kc] + "_%d" % tt for tt in range(4)] for kc in range(8)]
        if l == 0:
            dump("hnT%d" % g, hnT[:], [128, 8, 512], [k for ks in HKall for k in ks], BF16)
        ck(1)

        def fm_block(b, consumer):
            for cc in range(4):
                bank = rotate("fm", [0, 1, 2, 3])
                for kc in range(8):
                    P.pe(lambda e, cc=cc, kc=kc, bank=bank: e.matmul(ps[bank][:], wview[b][:, kc, cc * 128:(cc + 1) * 128],
                                                                     hnT[:, kc, :], start=(kc == 0), stop=(kc == 7)),
                         reads=["WB%d" % b] + HKall[kc], writes=["ps%d" % bank])
                consumer(cc, bank)

        def tm_block(b, W, consumer):
            for tt in range(4):
                bank = rotate("tm", [4, 5])
                for kc in range(8):
                    P.pe(lambda e, tt=tt, kc=kc, bank=bank: e.matmul(ps[bank][:, 0:W], hnT[:, kc, tt * 128:(tt + 1) * 128],
                                                                     wview[b][:, kc, 0:W], start=(kc == 0), stop=(kc == 7)),
                         reads=["WB%d" % b, HK[kc] + "_%d" % tt], writes=["ps%d" % bank])
                consumer(tt, bank)

        b_kv = b_kv_pre if b_kv_pre is not None else load_win(l, OFF_KV, 288)
        b_cu = load_win(l, OFF_CU, 512)
        b_ac = load_win(l, OFF_AC, 512)

        def kv_cons(tt, bank):
            P.act(lambda e: e.activation(out=lat[:, tt, :], in_=ps[bank][:, 0:288], func=AF.Copy),
                  reads=["ps%d" % bank], writes=["lat%d" % tt])
            P.act(lambda e: e.activation(out=junk[:, 0:256], in_=ps[bank][:, 0:256], func=AF.Square,
                                         accum_out=st_ss[:, tt:tt + 1]), reads=["ps%d" % bank], writes=["junk", "ss%d" % tt])
        tm_block(b_kv, 288, kv_cons)
        wb_release(b_kv)
        P.dve(lambda e: e.tensor_scalar(out=st_a[:], in0=st_ss[:], scalar1=1.0 / 256, scalar2=EPS,
                                        op0=ALU.mult, op1=ALU.add), reads=["ss%d" % t for t in range(4)], writes=["st_a"])
        rsqrt(st_y[:], st_a[:], 4, ["st_a"], "st_y")
        for tt in range(4):
            P.dve(lambda e, tt=tt: e.scalar_tensor_tensor(out=lat[:, tt, 0:256], in0=lat[:, tt, 0:256],
                                                          scalar=st_y[:, tt:tt + 1], in1=gkv_bc[:],
                                                          op0=ALU.mult, op1=ALU.mult),
                  reads=["lat%d" % tt, "st_y", "gkv_bc"], writes=["lat%d" % tt])
        LK = ["lat%d" % t for t in range(4)]
        if samp:
            kr5 = lat[:, :, 256:288].rearrange("p t (a j i) -> p t a j i", a=2, j=2)
            r15 = rk1[:].rearrange("p t (a j i) -> p t a j i", a=2, j=2)
            r25 = rk2[:].rearrange("p t (a j i) -> p t a j i", a=2, j=2)
            sn5 = ropek[:, 1, :, :].rearrange("p t (a j i) -> p t a j i", a=2, j=2)
            P.dve(lambda e: e.tensor_tensor(out=rk1[:], in0=lat[:, :, 256:288], in1=ropek[:, 0, :, :], op=ALU.mult),
                  reads=LK + ["ropek"], writes=["rk1"])
            P.dve(lambda e: e.tensor_tensor(out=r25[:, :, :, 0, :], in0=kr5[:, :, :, 1, :], in1=sn5[:, :, :, 0, :], op=ALU.mult),
                  reads=LK + ["ropek"], writes=["rk2a"])
            P.dve(lambda e: e.tensor_tensor(out=r25[:, :, :, 1, :], in0=kr5[:, :, :, 0, :], in1=sn5[:, :, :, 1, :], op=ALU.mult),
                  reads=LK + ["ropek"], writes=["rk2b"])
            P.dve(lambda e: e.tensor_tensor(out=lat[:, :, 256:288], in0=rk1[:], in1=rk2[:], op=ALU.add),
                  reads=["rk1", "rk2a", "rk2b"], writes=LK)
        else:
            outs.append(P.dma("sp", lambda e: e.dma_start(out=lat_o[l].rearrange("(t p) f -> p t f", p=128), in_=lat[:]),
                              reads=LK))
        if l == 0:
            dump("lat%d" % g, lat[:], [128, 4, 288], LK)
        ck(2)
        P.dve(lambda e: e.tensor_copy(out=lat_bf[:], in_=lat[:]), reads=LK, writes=["lat_bf"])
        oc = 1792 if samp else 0
        ock = ["ck%d" % (oc // 256), "ck%d" % (oc // 256 + 1)]
        okr = ["kr%d" % (oc // 256), "kr%d" % (oc // 256 + 1)]
        bank = rotate("tr", [6, 7])
        bank2 = rotate("tr", [6, 7])
        for tt in range(4):
            for ch in range(2):
                P.pe(lambda e, tt=tt, ch=ch: e.transpose(psb[bank][:, ch * 512 + tt * 128:ch * 512 + (tt + 1) * 128],
                                                         lat_bf[:, tt, ch * 128:(ch + 1) * 128], ident[:]),
                     reads=["lat_bf", "ident"], writes=["ps%d" % bank])
            P.pe(lambda e, tt=tt: e.transpose(psb[bank2][0:96, tt * 128:(tt + 1) * 128], lat_bf[:, tt, 192:288], ident[:]),
                 reads=["lat_bf", "ident"], writes=["ps%d" % bank2])
        P.act(lambda e: e.activation(out=ckvT[:, 0, oc:oc + 512], in_=psb[bank][:, 0:512], func=AF.Copy),
              reads=["ps%d" % bank], writes=[k + "_0" for k in ock])
        P.dve(lambda e: e.tensor_copy(out=ckvT[:, 1, oc:oc + 512], in_=psb[bank][:, 512:1024]),
              reads=["ps%d" % bank], writes=[k + "_1" for k in ock])
        P.act(lambda e: e.activation(out=KR[64:96, oc:oc + 512], in_=psb[bank2][64:96, 0:512], func=AF.Copy),
              reads=["ps%d" % bank2], writes=okr)
        if samp:
            P.dma("sp", lambda e: e.dma_start(out=aginA[l][0:256, :].rearrange("(c p) t -> p c t", p=128),
                                              in_=ckvT[:, :, oc:oc + 512]),
                  reads=[k + "_0" for k in ock] + [k + "_1" for k in ock], writes=["aginA%da" % l])
            P.dma("sp", lambda e: e.dma_start(out=aginA[l][256:288, :], in_=KR[64:96, oc:oc + 512]), reads=okr, writes=["aginA%db" % l])
            P.cc(lambda e: e.collective_compute("AllGather", ALU.bypass, replica_groups=[[0, 1, 2, 3], [4, 5, 6, 7]],
                                                ins=[aginA[l]], outs=[agoutA[l]]),
                 reads=["aginA%da" % l, "aginA%db" % l], writes=["agoutA%d" % l])

        P.dve(lambda e: e.memset(uT, 0.0), writes=["ar_c0", "ar_c1", "ar_c2"])

        def cu_cons(tt, bank):
            P.act(lambda e: e.activation(out=cu_tm[:, tt, :], in_=ps[bank][:], func=AF.Copy),
                  reads=["ps%d" % bank], writes=["ar_b"])
        tm_block(b_cu, 512, cu_cons)
        wb_release(b_cu)
        b_ax = load_win(l, OFF_AX, 512)

        def ac_cons(cc, bank):
            P.act(lambda e: e.activation(out=acT[:, cc, :], in_=ps[bank][:], func=AF.Copy),
                  reads=["ps%d" % bank], writes=["ar_a"])
        fm_block(b_ac, ac_cons)
        if samp:
            bank = rotate("tm", [4, 5])
            for kc in range(8):
                P.pe(lambda e, kc=kc, bank=bank: e.matmul(ps[bank][0:2, :], hnT[:, kc, 0:512:511], wview[b_ac][:, kc, :],
                                                          start=(kc == 0), stop=(kc == 7)),
                     reads=["WB%d" % b_ac] + HKall[kc], writes=["ps%d" % bank])
            P.act(lambda e, bank=bank: e.activation(out=uha, in_=ps[bank][0:2, :], func=AF.Copy),
                  reads=["ps%d" % bank], writes=["ttmp1"])
        wb_release(b_ac)
        b_q = load_win(l, OFF_Q, 384)

        def ax_cons(cc, bank):
            if samp:
                dst = uT[:, cc, 1:513]
                P.dve(lambda e: e.tensor_tensor(out=dst, in0=acT[:, cc, :], in1=ps[bank][:], op=ALU.mult),
                      reads=["ps%d" % bank, "ar_a"], writes=["ar_c0", "ar_c1", "ar_c2"])
            else:
                dst = uT[:, cc, 0:516].rearrange("p (s t) -> p s t", s=2)[:, :, 1:257]
                P.dve(lambda e: e.tensor_tensor(out=dst, in0=acT[:, cc, :].rearrange("p (s t) -> p s t", s=2),
                                                in1=ps[bank][:].rearrange("p (s t) -> p s t", s=2), op=ALU.mult),
                      reads=["ps%d" % bank, "ar_a"], writes=["ar_c0", "ar_c1", "ar_c2"])
        fm_block(b_ax, ax_cons)
        if samp:
            bank = rotate("tm", [4, 5])
            for kc in range(8):
                P.pe(lambda e, kc=kc, bank=bank: e.matmul(ps[bank][0:2, :], hnT[:, kc, 0:512:511], wview[b_ax][:, kc, :],
                                                          start=(kc == 0), stop=(kc == 7)),
                     reads=["WB%d" % b_ax] + HKall[kc], writes=["ps%d" % bank])
            P.dve(lambda e, bank=bank: e.tensor_tensor(out=uh[:], in0=uha, in1=ps[bank][0:2, :], op=ALU.mult),
                  reads=["ps%d" % bank, "ttmp1"], writes=["uh"])
        wb_release(b_ax)
        b_az = load_win(l, OFF_AZ, 512)
        ck(3)

        if samp:
            P.dma("sp", lambda e: e.dma_start(out=aginB[l][0:8, :], in_=cu_tm[0:8, 0, :]), reads=["ar_b"], writes=["aginB%da" % l])
            P.dma("sp", lambda e: e.dma_start(out=aginB[l][8:16, :], in_=cu_tm[120:128, 3, :]), reads=["ar_b"], writes=["aginB%db" % l])
            P.dma("sp", lambda e: e.dma_start(out=aginB[l][16:18, :], in_=uh[:]), reads=["uh"], writes=["aginB%dc" % l])
            P.cc(lambda e: e.collective_compute("AllGather", ALU.bypass, replica_groups=[[0, 1, 2, 3], [4, 5, 6, 7]],
                                                ins=[aginB[l]], outs=[agoutB[l]]),
                 reads=["aginB%d%s" % (l, x) for x in "abc"], writes=["agoutB%d" % l])
        def q_cons(tt, bank):
            P.act(lambda e: e.activation(out=junk[:, 0:384], in_=ps[bank][:, 0:384], func=AF.Square,
                                         accum_out=st_ss[:, tt:tt + 1]), reads=["ps%d" % bank], writes=["junk", "ss%d" % tt])
            P.dve(lambda e: e.tensor_copy(out=qn[:, tt, :], in_=ps[bank][:, 0:384]), reads=["ps%d" % bank], writes=["qn%d" % tt])
        tm_block(b_q, 384, q_cons)
        wb_release(b_q)
        b_ab = load_win(l, OFF_AB, 512)
        P.dve(lambda e: e.tensor_scalar(out=st_a[:], in0=st_ss[:], scalar1=1.0 / 384, scalar2=EPS,
                                        op0=ALU.mult, op1=ALU.add), reads=["ss%d" % t for t in range(4)], writes=["st_a"])
        rsqrt(st_y[:], st_a[:], 4, ["st_a"], "st_y")
        for tt in range(4):
            P.dve(lambda e, tt=tt: e.scalar_tensor_tensor(out=qn[:, tt, :], in0=qn[:, tt, :], scalar=st_y[:, tt:tt + 1],
                                                          in1=gq_bc[:], op0=ALU.mult, op1=ALU.mult),
                  reads=["qn%d" % tt, "st_y", "gq_bc"], writes=["qn%d" % tt])
        bank = rotate("tr", [6, 7])
        bank2 = rotate("tr", [6, 7])
        for tt in range(4):
            for ch in range(3):
                bk = bank if ch < 2 else bank2
                co = (ch % 2) * 512 + tt * 128
                P.pe(lambda e, tt=tt, ch=ch, bk=bk, co=co: e.transpose(psb[bk][:, co:co + 128], qn[:, tt, ch * 128:(ch + 1) * 128], ident[:]),
                     reads=["qn%d" % tt, "ident"], writes=["ps%d" % bk])
        P.act(lambda e: e.activation(out=qnT[:, 0, :], in_=psb[bank][:, 0:512], func=AF.Copy), reads=["ps%d" % bank], writes=["qnT0"])
        P.dve(lambda e: e.tensor_copy(out=qnT[:, 1, :], in_=psb[bank][:, 512:1024]), reads=["ps%d" % bank], writes=["qnT1"])
        P.act(lambda e: e.activation(out=qnT[:, 2, :], in_=psb[bank2][:, 0:512], func=AF.Copy), reads=["ps%d" % bank2], writes=["qnT2"])

        ck(4)
        def az_cons(cc, bank):
            s = rotate("tht", [0, 1])
            P.act(lambda e: e.activation(out=tht[s][:], in_=ps[bank][:], func=AF.Tanh, scale=0.5),
                  reads=["ps%d" % bank], writes=["tht%d" % s])
            P.dve(lambda e: e.scalar_tensor_tensor(out=ga[:, cc, :], in0=tht[s][:], scalar=1.0, in1=ps[bank][:],
                                                   op0=ALU.add, op1=ALU.mult),
                  reads=["tht%d" % s, "ps%d" % bank], writes=["ga%d" % cc])
        fm_block(b_az, az_cons)
        wb_release(b_az)
        b_bz = load_win(l, OFF_BZ, 512)

        def ab_cons(cc, bank):
            P.dve(lambda e: e.tensor_tensor(out=ga[:, cc, :], in0=ga[:, cc, :], in1=ps[bank][:], op=ALU.mult),
                  reads=["ga%d" % cc, "ps%d" % bank], writes=["ga%d" % cc])
        fm_block(b_ab, ab_cons)
        wb_release(b_ab)
        b_cz = load_win(l, OFF_CZ, 512)

        def gate_cons(dst, key):
            def cons(cc, bank):
                s = rotate("tht", [0, 1])
                P.act(lambda e: e.activation(out=tht[s][:], in_=ps[bank][:], func=AF.Tanh, scale=0.5),
                      reads=["ps%d" % bank], writes=["tht%d" % s])
                P.dve(lambda e: e.scalar_tensor_tensor(out=dst[:, cc, :], in0=tht[s][:], scalar=1.0, in1=ps[bank][:],
                                                       op0=ALU.add, op1=ALU.mult),
                      reads=["tht%d" % s, "ps%d" % bank], writes=[key + "%d" % cc])
            return cons
        fm_block(b_bz, gate_cons(gb, "gb"))
        wb_release(b_bz)
        fm_block(b_cz, gate_cons(gc, "gc"))
        wb_release(b_cz)

        ck(5)
        if not consts_loaded:
            consts_loaded.append(True)
            P.dma("pool", lambda e: e.dma_start(out=poolp[:], in_=c_poolp), writes=["poolp"])
            P.dma("pool", lambda e: e.dma_start(out=pools[:], in_=c_pools), writes=["pools"])
            P.dma("pool", lambda e: e.dma_start(out=halo[:], in_=c_halo), writes=["halo"])
            P.dma("pool", lambda e: e.dma_start(out=selc[:], in_=c_selc), writes=["selc"])
        P.dma("pool", lambda e: e.dma_start(out=wq[:].rearrange("p c w -> p (c w)"), in_=pk(l, "wuq")), writes=["wq"])
        P.dma("pool", lambda e: e.dma_start(out=wkv[:].rearrange("p c w -> p (c w)"), in_=pk(l, "wukv")), writes=["wkv"])
        P.dma("pool", lambda e: e.dma_start(out=wpool[:].rearrange("p g d -> p (g d)"), in_=pk(l, "poolw")), writes=["wpool"])
        if samp:
            P.act(lambda e: e.activation(out=wqs[:], in_=wq[:], func=AF.Copy), reads=["wq"], writes=["wqs"])
            v6 = lambda t: t[:].rearrange("p c (h d) -> p (c h) d", d=96)[:, :, 64:96].rearrange("p n (a j i) -> p n a j i", a=2, j=2)
            P.dve(lambda e: e.tensor_copy(out=v6(wqs)[:, :, :, 0, :], in_=v6(wq)[:, :, :, 1, :]), reads=["wq", "wqs"], writes=["wqs"])
            P.dve(lambda e: e.tensor_copy(out=v6(wqs)[:, :, :, 1, :], in_=v6(wq)[:, :, :, 0, :]), reads=["wq", "wqs"], writes=["wqs"])

        if samp:
            ak = "agoutA%d" % l
            for ch in range(2):
                src = agoutA[l].rearrange("(r f) t -> f r t", r=4)[ch * 128:(ch + 1) * 128]
                P.dma("sp", lambda e, ch=ch, src=src: e.dma_start(
                    out=ckvT[:, ch, 256:NK_S].rearrange("p (r t) -> p r t", r=4), in_=src),
                    reads=[ak], writes=["ck%d_%d" % (i, ch) for i in range(1, 9)])
            src = agoutA[l].rearrange("(r f) t -> f r t", r=4)[256:288]
            P.dma("sp", lambda e: e.dma_start(out=KR[64:96, 256:NK_S].rearrange("p (r t) -> p r t", r=4), in_=src),
                  reads=[ak], writes=["kr%d" % i for i in range(1, 9)])
            P.dma("sp", lambda e: e.dma_start(out=H[:, :], in_=agoutB[l]), reads=["agoutB%d" % l], writes=["H"])
            bank = rotate("misc", [4, 5])
            for cc in range(4):
                P.pe(lambda e, cc=cc, bank=bank: e.matmul(ps[bank][:, cc * 2:cc * 2 + 2], H[:, cc * 128:(cc + 1) * 128], selc[:],
                                                          start=(cc == 0), stop=True, skip_group_check=True),
                     reads=["H", "selc"], writes=["ps%d" % bank])
            P.dve(lambda e, bank=bank: e.tensor_copy(out=uT[:, :, 0:514:513],
                                                     in_=ps[bank][:, 0:8].rearrange("p (c s) -> p c s", c=4)),
                  reads=["ps%d" % bank], writes=["ar_c0", "ar_c1", "ar_c2"])

        ck(6)
        if g == 0 and l == 0:
            emit_mod(1)
        for cc in range(4):
            bank = rotate("cv", [0, 1])
            segs = [(0, 512, 0)] if samp else [(0, 256, 0), (256, 256, 258)]
            first = True
            for (t0, n, e0) in segs:
                for k in range(3):
                    P.pe(lambda e, cc=cc, k=k, t0=t0, n=n, e0=e0, bank=bank, first=first: e.matmul(
                        ps[bank][:, t0:t0 + n], Dg[:, cc * 3 + k, :], uT[:, cc, e0 + k:e0 + k + n],
                        start=first, stop=(k == 2), skip_group_check=True),
                        reads=["Dg", "ar_c0", "ar_c1", "ar_c2"], writes=["ps%d" % bank])
                    first = False
            P.dve(lambda e, cc=cc, bank=bank: e.scalar_tensor_tensor(out=ga[:, cc, :], in0=ps[bank][:], scalar=cbT[:, cc:cc + 1],
                                                                     in1=ga[:, cc, :], op0=ALU.add, op1=ALU.mult),
                  reads=["ps%d" % bank, "cbT", "ga%d" % cc], writes=["ga%d" % cc])
        if l == 0:
            dump("ya%d" % g, ga[:], [128, 4, 512], ["ga%d" % c for c in range(4)], BF16)

        ck(7)
        pb = pools if samp else poolp
        pbk = "pools" if samp else "poolp"
        if samp:
            pairs = [(i, j) for j in range(4) for i in range(4) if abs(i - j) <= 1]
        else:
            pairs = [(0, 0), (1, 0), (0, 1), (1, 1), (2, 2), (3, 2), (2, 3), (3, 3)]
        for gi in range(4):
            bank = rotate("pl", [2, 3])
            first = True
            for j in range(4):
                terms = []
                for (i, jj) in pairs:
                    if jj != j:
                        continue
                    if samp:
                        idx = gi * 10 + [p_ for p_ in pairs].index((i, j))
                    else:
                        idx = gi * 4 + [(0, 0), (1, 0), (0, 1), (1, 1)].index((i % 2, j % 2))
                    terms.append((cu_tm[:, i, gi * 128:(gi + 1) * 128], pb[:, idx, :], ["ar_b", pbk]))
                if samp and j in (0, 3):
                    terms.append((H[:, gi * 128:(gi + 1) * 128], halo[:, gi * 2 + (0 if j == 0 else 1), :], ["H", "halo"]))
                for ti, (lh, rh, rk) in enumerate(terms):
                    P.pe(lambda e, lh=lh, rh=rh, j=j, bank=bank, first=first, lastt=(ti == len(terms) - 1): e.matmul(
                        ps[bank][:, j * 128:(j + 1) * 128], lh, rh, start=first, stop=lastt, skip_group_check=True),
                        reads=rk, writes=["ps%d" % bank])
                    first = False
            s = gi % 2
            P.act(lambda e, s=s, bank=bank: e.activation(out=pooledT[s][:], in_=ps[bank][:], func=AF.Copy),
                  reads=["ps%d" % bank], writes=["pooledT%d" % s])
            bank2 = rotate("mx", [4, 5])
            P.pe(lambda e, gi=gi, s=s, bank2=bank2: e.matmul(ps[bank2][:], wpool[:, gi, :], pooledT[s][:], start=True, stop=True),
                 reads=["wpool", "pooledT%d" % s], writes=["ps%d" % bank2])
            P.dve(lambda e, gi=gi, bank2=bank2: e.scalar_tensor_tensor(out=gc[:, gi, :], in0=ps[bank2][:], scalar=psT[:, gi:gi + 1],
                                                                       in1=gc[:, gi, :], op0=ALU.mult, op1=ALU.mult),
                  reads=["ps%d" % bank2, "psT", "gc%d" % gi], writes=["gc%d" % gi])
        if l == 0:
            dump("yc%d" % g, gc[:], [128, 4, 512], ["gc%d" % c for c in range(4)], BF16)

        ck(8)
        if samp:
            probs = [(0, 512, 0, NK_S)]
            nk_tot = NK_S
        else:
            probs = [(0, 256, 0, 256), (256, 256, 256, 256)]
            nk_tot = 512
        nkb = (nk_tot + 255) // 256
        CKR = lambda ch: ["ck%d_%d" % (i, ch) for i in range(nkb)]
        KRR = ["kr%d" % i for i in range(nkb)]
        nkt = nk_tot // 128

        def head_prep(h):
            s = h % 2
            ktk = ["ar_a", "ar_b"][s]
            steps = []
            c0 = 0
            while c0 < nk_tot:
                n = min(512, nk_tot - c0)

                def kstep(c0=c0, n=n):
                    bank = rotate("ku", [5])
                    for ch in range(2):
                        P.pe(lambda e, ch=ch: e.matmul(ps[bank][0:64, 0:n], wkv[:, ch, h * 128:h * 128 + 64],
                                                       ckvT[:, ch, c0:c0 + n], start=(ch == 0), stop=(ch == 1)),
                             reads=["wkv"] + CKR(ch), writes=["ps%d" % bank])
                    if samp:
                        P.dve(lambda e: e.tensor_copy(out=KT[s][0:64, c0:c0 + n], in_=ps[bank][0:64, 0:n]),
                              reads=["ps%d" % bank], writes=[ktk])
                    else:
                        P.act(lambda e: e.activation(out=KT[s][0:64, c0:c0 + n], in_=ps[bank][0:64, 0:n], func=AF.Copy),
                              reads=["ps%d" % bank], writes=[ktk])
                steps.append(kstep)
                c0 += n

            def rstep():
                P.dve(lambda e: e.tensor_copy(out=KT[s][64:96, 0:nk_tot], in_=KR[64:96, 0:nk_tot]), reads=KRR, writes=[ktk])
            steps.append(rstep)
            vo = 0 if s == 0 else 64
            vbank = {}
            t0 = 0
            while t0 < nkt:
                n = min(8, nkt - t0)
                for half in range(2):
                    def vstep(t0=t0, n=n, half=half):
                        if half == 0:
                            vbank[t0] = rotate("vu", [6])
                        bank = vbank[t0]
                        lo, hi = (0, (n + 1) // 2) if half == 0 else ((n + 1) // 2, n)
                        for r in range(lo, hi):
                            for ch in range(2):
                                P.pe(lambda e, r=r, ch=ch: e.matmul(
                                    ps[bank][:, r * 64:(r + 1) * 64], ckvT[:, ch, (t0 + r) * 128:(t0 + r + 1) * 128],
                                    wkv[:, ch, h * 128 + 64:h * 128 + 128], start=(r == 0 and ch == 0), stop=(ch == 1),
                                    skip_group_check=True),
                                    reads=["wkv"] + CKR(ch), writes=["ps%d" % bank])
                        if half == 1:
                            P.dve(lambda e: e.tensor_copy(
                                out=Vh[s][:, t0:t0 + n, vo:vo + 64], in_=ps[bank][:, 0:n * 64].rearrange("p (r d) -> p r d", d=64)),
                                reads=["ps%d" % bank], writes=["Vh%d" % s])
                    steps.append(vstep)
                t0 += n

            def qstep():
                bank = rotate("qu", [7])
                for ch in range(3):
                    P.pe(lambda e, ch=ch: e.matmul(ps[bank][0:96, :], wq[:, ch, h * 96:(h + 1) * 96], qnT[:, ch, :],
                                                   start=(ch == 0), stop=(ch == 2)),
                         reads=["wq", "qnT%d" % ch], writes=["ps%d" % bank])
                if samp:
                    P.dve(lambda e: e.tensor_tensor(out=qtmp[0][:], in0=ps[bank][0:96, :], in1=ropeq[:, 0, :], op=ALU.mult),
                          reads=["ps%d" % bank, "ropeq"], writes=["qtmp0"])
                else:
                    P.act(lambda e: e.activation(out=QT[s][:], in_=ps[bank][0:96, :], func=AF.Copy),
                          reads=["ps%d" % bank], writes=["QT%d" % s])
            steps.append(qstep)
            if samp:
                def qstep2():
                    bank2 = rotate("qu2", [5])
                    for ch in range(3):
                        P.pe(lambda e, ch=ch: e.matmul(ps[bank2][0:96, :], wqs[:, ch, h * 96:(h + 1) * 96], qnT[:, ch, :],
                                                       start=(ch == 0), stop=(ch == 2)),
                             reads=["wqs", "qnT%d" % ch], writes=["ps%d" % bank2])
                    P.dve(lambda e: e.tensor_tensor(out=qtmp[1][:], in0=ps[bank2][0:96, :], in1=ropeq[:, 1, :], op=ALU.mult),
                          reads=["ps%d" % bank2, "ropeq"], writes=["qtmp1"])
                    P.dve(lambda e: e.tensor_tensor(out=QT[s][:], in0=qtmp[0][:], in1=qtmp[1][:], op=ALU.add),
                          reads=["qtmp0", "qtmp1"], writes=["QT%d" % s])
                steps.append(qstep2)
            return steps

        LA = 2
        tiles_h = [(q0, nq, k0, kt, kt == nk // 128 - 1) for (q0, nq, k0, nk) in probs for kt in range(nk // 128)]
        n_t = len(tiles_h)
        for st_ in head_prep(0):
            st_()
        pvq = []
        prep_q = []
        timers = []
        ofirst = {}

        def emit_pv(item):
            (h, q0, nq, k0, kt, lastk, pslot, last_of_head) = item
            s = h % 2
            obank = 3 + s
            M = 65 if s == 0 else 128
            P.pe(lambda e: e.matmul(
                ps[obank][0:M, q0:q0 + nq], Vh[s][:, k0 // 128 + kt, 0:M], PT[pslot][:, 0:nq],
                start=(h not in ofirst), stop=lastk, skip_group_check=True),
                reads=["ar_c%d" % pslot, "Vh%d" % s], writes=["ps%d" % obank])
            ofirst[h] = True
            if last_of_head:
                p0 = 64 if s == 0 else 0
                ro = 0 if s == 0 else 64
                cc = h // 2
                P.dve(lambda e: e.reciprocal(out=rcp[s][p0:p0 + 1, :], in_=ps[obank][p0:p0 + 1, :]),
                      reads=["ps%d" % obank], writes=["rcp%d" % s])

                def norm2():
                    bcb = rotate("bc", [6])
                    if s == 0:
                        P.pe(lambda e: e.matmul(ps[bcb][0:64, :], onesel[64:65, 0:64], rcp[s][64:65, :], start=True, stop=True),
                             reads=["onesel", "rcp%d" % s], writes=["ps%d" % bcb])
                    else:
                        P.pe(lambda e: e.matmul(ps[bcb][:, :], onesel[0:1, :], rcp[s][0:1, :], start=True, stop=True),
                             reads=["onesel", "rcp%d" % s], writes=["ps%d" % bcb])
                    P.dve(lambda e: e.tensor_tensor(out=gb[ro:ro + 64, cc, :], in0=ps[obank][ro:ro + 64, :], in1=gb[ro:ro + 64, cc, :], op=ALU.mult),
                          reads=["ps%d" % obank, "gb%d" % cc], writes=["gb%d" % cc])
                    P.dve(lambda e: e.tensor_tensor(out=gb[ro:ro + 64, cc, :], in0=gb[ro:ro + 64, cc, :], in1=ps[bcb][ro:ro + 64, :], op=ALU.mult),
                          reads=["ps%d" % bcb, "gb%d" % cc], writes=["gb%d" % cc])
                timers.append([min(8, n_t), norm2])

        for h in range(8):
            s = h % 2
            ktk = ["ar_a", "ar_b"][s]
            while prep_q:
                prep_q.pop(0)()
            nxt_steps = head_prep(h + 1) if h < 7 else []
            per_iter = (len(nxt_steps) + max(1, n_t - LA) - 1) // max(1, n_t - LA) if nxt_steps else 0
            for it, (q0, nq, k0, kt, lastk) in enumerate(tiles_h):
                sbank = rotate("S", [0, 1, 2])
                P.pe(lambda e: e.matmul(
                    ps[sbank][:, 0:nq], KT[s][:, k0 + kt * 128:k0 + (kt + 1) * 128], QT[s][:, q0:q0 + nq], start=True, stop=True),
                    reads=[ktk, "QT%d" % s], writes=["ps%d" % sbank])
                pslot = rotate("PT", [0, 1, 2])
                P.act(lambda e: e.activation(
                    out=PT[pslot][:, 0:nq], in_=ps[sbank][:, 0:nq], func=AF.Exp, scale=SM_SCALE),
                    reads=["ps%d" % sbank], writes=["ar_c%d" % pslot])
                pvq.append((h, q0, nq, k0, kt, lastk, pslot, it == n_t - 1))
                if len(pvq) > LA:
                    emit_pv(pvq.pop(0))
                for tm in list(timers):
                    tm[0] -= 1
                    if tm[0] <= 0:
                        timers.remove(tm)
                        tm[1]()
                if it == LA - 1:
                    prep_q = nxt_steps
                if it >= LA:
                    for _ in range(per_iter):
                        if prep_q:
                            prep_q.pop(0)()
        while pvq:
            emit_pv(pvq.pop(0))
        for tm in timers:
            tm[1]()
        if l == 0:
            dump("yb%d" % g, gb[:], [128, 4, 512], ["gb%d" % c for c in range(4)], BF16)

        ck(9)
        if nxt is not None:
            load_layer_small(nxt[0], nxt[1])
        ysrc = [(ga, "ga"), (gb, "gb"), (gc, "gc")]
        for dc in range(8):
            b = wb_alloc()
            sbr = dc % 2
            P.dma("pool", lambda e, b=b, dc=dc: e.dma_start(out=WB[b][:, 0:3072], in_=pk(l, "gate%d" % dc)),
                  reads=(["agoutA%d" % l] if (samp and dc == 0) else []), writes=["WB%d" % b])
            P.dma("pool", lambda e, sbr=sbr, dc=dc: e.dma_start(out=wbr[sbr][:].rearrange("p a w -> p (a w)"), in_=pk(l, "br%d" % dc)),
                  writes=["wbr%d" % sbr])
            gv = WB[b][:, 0:3072].rearrange("p (k n w) -> p k n w", k=8, n=3)
            for n in range(3):
                for kc in range(8):
                    P.pe(lambda e, gv=gv, n=n, kc=kc: e.matmul(ps[n][:], gv[:, kc, n, :], hnT[:, kc, :],
                                                             start=(kc == 0), stop=(kc == 7)),
                         reads=["WB%d" % b] + HKall[kc], writes=["ps%d" % n])
                P.act(lambda e, n=n: e.activation(out=thg[n][:], in_=ps[n][:], func=AF.Tanh, scale=0.5),
                      reads=["ps%d" % n], writes=["thg%d" % n])
            wb_release(b)
            for n in range(3):
                yt, yk = ysrc[n]
                for cc in range(4):
                    P.pe(lambda e, n=n, cc=cc, sbr=sbr, yt=yt: e.matmul(ps[3 + n][:], wbr[sbr][:, n * 4 + cc, :], yt[:, cc, :],
                                                                        start=(cc == 0), stop=(cc == 3)),
                         reads=["wbr%d" % sbr, yk + "%d" % cc], writes=["ps%d" % (3 + n)])
            P.dve(lambda e: e.scalar_tensor_tensor(out=tn[0][:], in0=thg[0][:], scalar=1.0, in1=ps[3][:], op0=ALU.add, op1=ALU.mult),
                  reads=["thg0", "ps3"], writes=["tn0"])
            P.dve(lambda e: e.scalar_tensor_tensor(out=tn[1][:], in0=thg[1][:], scalar=1.0, in1=ps[4][:], op0=ALU.add, op1=ALU.mult),
                  reads=["thg1", "ps4"], writes=["tn1"])
            P.dve(lambda e: e.tensor_tensor(out=tn[0][:], in0=tn[0][:], in1=tn[1][:], op=ALU.add), reads=["tn0", "tn1"], writes=["tn0"])
            P.dve(lambda e: e.scalar_tensor_tensor(out=tn[1][:], in0=thg[2][:], scalar=1.0, in1=ps[5][:], op0=ALU.add, op1=ALU.mult),
                  reads=["thg2", "ps5"], writes=["tn1"])
            P.dve(lambda e, dc=dc: e.tensor_tensor(out=mrgT[:, dc, :], in0=tn[0][:], in1=tn[1][:], op=ALU.add),
                  reads=["tn0", "tn1"], writes=["mg%d" % dc])
        if l == 0:
            dump("mrg%d" % g, mrgT[:], [128, 8, 512], ["mg%d" % c for c in range(8)], BF16)

        ck(10)

    tail_state = {}

    def tail_setup(l, g):
        P.dma("sp", lambda e: e.dma_start(out=GG[:].rearrange("p (r j) -> p r j", r=4),
                                          in_=modsrc(l, g, 2).rearrange("r h p -> r (h p)").partition_broadcast(128)),
              reads=["agm_out%d" % l], writes=["GG"])
        bo = []
        for half in range(2):
            b = wb_alloc()
            P.dma("pool", lambda e, b=b, half=half: e.dma_start(out=WB[b][:, :], in_=pk(l, "wo%d" % half)),
                  writes=["WB%d" % b])
            bo.append(b)
        tail_state["bo"] = bo

    def tail_tile(l, g, tt, last):
        bo = tail_state["bo"]
        banks = [rotate("wo", [4, 5, 2, 3, 0, 1]) for _ in range(2)]
        for half in range(2):
            for kc in range(8):
                P.pe(lambda e, half=half, kc=kc, bank=banks[half]: e.matmul(
                    ps[bank][:], mrgT[:, kc, tt * 128:(tt + 1) * 128], WB8[bo[half]][:, kc, :], start=(kc == 0), stop=(kc == 7)),
                    reads=["mg%d" % kc, "WB%d" % bo[half]], writes=["ps%d" % banks[half]])
            P.act(lambda e, half=half, bank=banks[half]: e.activation(out=junk[:, 0:512], in_=ps[bank][:], func=AF.Square,
                                                                      accum_out=st_ss[:, half:half + 1]),
                  reads=["ps%d" % banks[half]], writes=["junk", "ss%d" % half])
        P.dve(lambda e: e.tensor_tensor(out=st_a[:, 0:1], in0=st_ss[:, 0:1], in1=st_ss[:, 1:2], op=ALU.add),
              reads=["ss0", "ss1"], writes=["st_a"])
        P.dve(lambda e: e.tensor_scalar(out=st_a[:, 0:1], in0=st_a[:, 0:1], scalar1=1.0 / 1024, scalar2=16.0 * EPS,
                                        op0=ALU.mult, op1=ALU.add), reads=["st_a"], writes=["st_a"])
        rsqrt(st_y[:, 0:1], st_a[:, 0:1], 1, ["st_a"], "st_y")
        for half in range(2):
            hs = slice(half * 512, (half + 1) * 512)
            P.dve(lambda e, half=half, hs=hs, bank=banks[half]: e.scalar_tensor_tensor(
                out=ttmp[half][:], in0=ps[bank][:], scalar=st_y[:, 0:1], in1=GG[:, hs], op0=ALU.mult, op1=ALU.mult),
                reads=["ps%d" % banks[half], "st_y", "GG"], writes=["ttmp%d" % half])
            P.dve(lambda e, half=half, hs=hs: e.tensor_tensor(out=X[:, tt, hs], in0=X[:, tt, hs], in1=ttmp[half][:], op=ALU.add),
                  reads=["ttmp%d" % half, "X%d" % tt], writes=["X%d" % tt])
        if last:
            outs.append(P.dma("sp", lambda e: e.dma_start(out=y_out[g][tt * 128:(tt + 1) * 128, :], in_=X[:, tt, :]),
                              reads=["X%d" % tt]))

    def tail_done(l, g):
        for b in tail_state["bo"]:
            wb_release(b)
        if l == 0:
            dump("x1_%d" % g, X[:], [128, 4, 1024], ["X%d" % t for t in range(4)])

    def cache_prep(l):
        P.dma("pool", lambda e: e.dma_start(out=cache_bf, in_=cache_d[l].rearrange("(t p) f -> p t f", p=128)),
              writes=["lat_bf"])
        bank = rotate("tr", [6, 7])
        bank2 = rotate("tr", [6, 7])
        for t in range(2):
            for ch in range(2):
                P.pe(lambda e, t=t, ch=ch: e.transpose(psb[bank][:, ch * 512 + t * 128:ch * 512 + (t + 1) * 128],
                                                       cache_bf[:, t, ch * 128:(ch + 1) * 128], ident[:]),
                     reads=["lat_bf", "ident"], writes=["ps%d" % bank])
            P.pe(lambda e, t=t: e.transpose(psb[bank2][0:96, t * 128:(t + 1) * 128], cache_bf[:, t, 192:288], ident[:]),
                 reads=["lat_bf", "ident"], writes=["ps%d" % bank2])
        P.act(lambda e: e.activation(out=ckvT[:, 0, 0:256], in_=psb[bank][:, 0:256], func=AF.Copy),
              reads=["ps%d" % bank], writes=["ck0_0"])
        P.dve(lambda e: e.tensor_copy(out=ckvT[:, 1, 0:256], in_=psb[bank][:, 512:768]), reads=["ps%d" % bank], writes=["ck0_1"])
        P.act(lambda e: e.activation(out=KR[64:96, 0:256], in_=psb[bank2][64:96, 0:256], func=AF.Copy),
              reads=["ps%d" % bank2], writes=["kr0"])

    try:
        emit_consts()
        ck(-2)
        passes = [(0, 0), (1, 0), (0, 1), (1, 1)]
        if stage is not None and stage >= 100:
            passes = [(0, 1), (1, 1)]

        def xload(g, tt):
            P.dma("sp", lambda e: e.dma_start(out=X[:, tt, :], in_=x_in[g][tt * 128:(tt + 1) * 128, :]), writes=["X%d" % tt])

        l0, g0 = passes[0]
        emit_mod(0)
        ck(-1)
        for tt in range(4):
            P.dma("pool", lambda e, tt=tt: e.dma_start(out=X[:, tt, :], in_=x_in[g0][tt * 128:(tt + 1) * 128, :]), writes=["X%d" % tt])
        for tt in range(4):
            prenorm_a(tt, 4 + tt)
        load_layer_small(l0, g0)
        for tt in range(4):
            prenorm_b(tt, 4 + tt)
        b_kv_pre = None
        for i, (l, g) in enumerate(passes):
            if g == 1:
                cache_prep(l)
            nxt = passes[i + 1] if i + 1 < len(passes) else None
            if stage is not None and stage % 100 == 50:
                nxt = None
            group_layer(l, g, b_kv_pre, nxt)
            last = (l == 1)
            tail_setup(l, g)
            b_kv_pre = None
            if nxt is not None:
                b_kv_pre = load_win(nxt[0], OFF_KV, 288)
            for tt in range(4):
                tail_tile(l, g, tt, last)
                if nxt is not None:
                    if nxt[1] != g:
                        xload(nxt[1], tt)
                    if tt >= 1:
                        prenorm_tile(nxt[0], nxt[1], tt - 1)
            if nxt is not None:
                prenorm_tile(nxt[0], nxt[1], 3)
            tail_done(l, g)
            if stage is not None and stage % 100 == 50:
                raise _Stop()
    except _Stop:
        pass
    stats = P.emit(final_wait_ops=[o.idx for o in outs])
    return nc, stats, dbg


def _pool_matrix(S):
    mats = []
    t = np.arange(S)
    for win in POOL_WINDOWS:
        lo = np.clip(t - win // 2, 0, S)
        hi = np.clip(t + win - win // 2, 0, S)
        A = np.zeros((S, S), np.float32)
        for tt in range(S):
            A[lo[tt]:hi[tt], tt] = 1.0 / float(hi[tt] - lo[tt])
        A[t, t] -= 1.0
        mats.append(A)
    return mats


def _core_consts(qd):
    c = {}
    c["c_ident"] = np.eye(128, dtype=np.float32)
    pos = qd * 512 + np.arange(512)
    row = (pos // 64).astype(np.float32)
    col = (pos % 64).astype(np.float32)
    inv = (1.0 / (10000.0 ** (np.arange(0, 16, 2, dtype=np.float32) / 16.0))).astype(np.float32)
    ang = np.stack([row[:, None] * inv, col[:, None] * inv], axis=1).astype(np.float32)
    cs, sn = np.cos(ang).astype(np.float32), np.sin(ang).astype(np.float32)
    cos2 = np.stack([cs, cs], axis=2).reshape(512, 32)
    sin2 = np.stack([-sn, sn], axis=2).reshape(512, 32)
    rk = np.stack([cos2, sin2], axis=0).reshape(2, 4, 128, 32).transpose(2, 0, 1, 3)
    c["c_ropek"] = np.ascontiguousarray(rk)
    rq = np.zeros((96, 2, 512), np.float32)
    rq[0:64, 0, :] = 1.0
    rq[64:96, 0, :] = cos2.T
    rq[64:96, 1, :] = sin2.T
    c["c_ropeq"] = rq
    Ap = _pool_matrix(256)
    pp = np.zeros((128, 16, 128), np.float32)
    for gi in range(4):
        for k, (i, j) in enumerate([(0, 0), (1, 0), (0, 1), (1, 1)]):
            pp[:, gi * 4 + k, :] = Ap[gi][i * 128:(i + 1) * 128, j * 128:(j + 1) * 128]
    c["c_poolp"] = pp
    As = _pool_matrix(2048)
    pairs = [(i, j) for j in range(4) for i in range(4) if abs(i - j) <= 1]
    psm = np.zeros((128, 40, 128), np.float32)
    base = qd * 512
    for gi in range(4):
        for k, (i, j) in enumerate(pairs):
            psm[:, gi * 10 + k, :] = As[gi][base + i * 128:base + (i + 1) * 128, base + j * 128:base + (j + 1) * 128]
    c["c_pools"] = psm
    hl = np.zeros((72, 8, 128), np.float32)
    sel = np.zeros((72, 2), np.float32)
    for gi in range(4):
        if qd > 0:
            for m in range(8):
                hl[(qd - 1) * 18 + 8 + m, gi * 2 + 0, :] = As[gi][base - 8 + m, base:base + 128]
        if qd < 3:
            for m in range(8):
                hl[(qd + 1) * 18 + m, gi * 2 + 1, :] = As[gi][base + 512 + m, base + 384:base + 512]
    if qd > 0:
        sel[(qd - 1) * 18 + 17, 0] = 1.0
    if qd < 3:
        sel[(qd + 1) * 18 + 16, 1] = 1.0
    c["c_halo"] = hl
    c["c_selc"] = sel
    return c


def _mod_vec(b_mod, g_pre, g_post, r):
    v = np.empty((2, 1280), np.float32)
    q = slice(r * 256, (r + 1) * 256)
    for l in range(2):
        b3 = b_mod[l].reshape(3, 1024)
        v[l, 0:256] = b3[0, q]
        v[l, 256:512] = b3[1, q]
        v[l, 512:768] = b3[2, q]
        v[l, 768:1024] = g_pre[l, q]
        v[l, 1024:1280] = g_post[l, q]
    return v


_CACHE = {}


def _get_prog(debug=False, stage=None):
    if (debug, stage) not in _CACHE:
        _CACHE[(debug, stage)] = build(debug, stage)
    return _CACHE[(debug, stage)]


def kernel(x_prompt, x_sample, cache_mla_latent, c, c_ctx, w_mod, b_mod, g_pre, g_post, w_in, conv_w, conv_b,
           g_q, w_uq, g_kv, w_ukv, pool_w, pool_scale, w_branch, w_o, _debug=False, _stage=None):
    f = lambda a: np.ascontiguousarray(np.asarray(a, dtype=np.float32))
    x_prompt, x_sample, cache_mla_latent, c, c_ctx = map(f, (x_prompt, x_sample, cache_mla_latent, c, c_ctx))
    shared = dict(b_mod=f(b_mod), g_pre=f(g_pre), g_post=f(g_post), conv_w=f(conv_w), conv_b=f(conv_b), g_q=f(g_q),
                  g_kv=f(g_kv), pool_scale=f(pool_scale),
                  wpk=_pack_weights(f(w_mod), f(w_in), f(w_uq), f(w_ukv), f(pool_w), f(w_branch), f(w_o)))
    nc, stats, dbg = _get_prog(_debug, _stage)
    in_maps = []
    wpk_rank = []
    for r in range(4):
        w = shared["wpk"].copy()
        _pack_mod(w, f(w_mod), r)
        wpk_rank.append(w)
    for i in range(8):
        b, qd = i // 4, i % 4
        m = dict(shared)
        m["wpk"] = wpk_rank[qd]
        m["modvec"] = _mod_vec(shared["b_mod"], shared["g_pre"], shared["g_post"], qd)
        m["xp"] = np.ascontiguousarray(x_prompt[2 * i:2 * i + 2].reshape(512, 1024))
        m["xs"] = np.ascontiguousarray(x_sample[b, qd * 512:(qd + 1) * 512])
        m["cache"] = np.ascontiguousarray(cache_mla_latent[b])
        m["cond"] = np.ascontiguousarray(np.stack([c_ctx, c[b]], axis=0))
        m.update(_core_consts(qd))
        in_maps.append(m)
    res = run_bass_kernel_spmd(nc, in_maps, core_ids=list(range(8)))
    R = res.results
    y_prompt = np.stack([R[i]["yp"].reshape(2, 256, 1024) for i in range(8)], 0).reshape(16, 256, 1024)
    y_sample = np.stack([R[i]["ys"] for i in range(8)], 0).reshape(2, 2048, 1024)
    st = np.stack([R[i]["lat"].reshape(2, 2, 256, 288).transpose(1, 0, 2, 3) for i in range(8)], 0).reshape(16, 2, 256, 288)
    outs = (y_prompt.astype(np.float32), y_sample.astype(np.float32), st.astype(np.float32))
    if _debug:
        return outs, R
    return outs
```
